# Optimizing a Trainium2 kernel written in Bass

```python
import math
import jax, jax.numpy as jnp
from jax import lax
import numpy as np

D_MODEL = 1024
BATCH = 32
SEQ = 2048
DEPTH = 4

DA_HEADS = 4
DA_HEAD_DIM = 64
DA_V_DIM = 2 * DA_HEAD_DIM
DA_QK_WIDTH = DA_HEADS * 2 * DA_HEAD_DIM
DA_WIDTH = DA_HEADS * DA_V_DIM
Q_BLOCK = 128
HB_HEADS = 4
HB_HEAD_DIM = 64
HB_WIDTH = HB_HEADS * HB_HEAD_DIM
MAX_NEG_LOGIT = 80.0
GC_HEADS = 4
GC_KEY_DIM = 32
GC_VAL_DIM = 64
GC_KEY_WIDTH = GC_HEADS * GC_KEY_DIM
GC_WIDTH = GC_HEADS * GC_VAL_DIM
GC_GATE_RANK = 16
GC_GATE_NORMALIZER = 16.0
CHUNK = 64
N_BRANCH = 3
D_FF = 4 * D_MODEL
N_MOD = 6
EPS = 1e-6

IN_SIZES = (DA_QK_WIDTH, DA_QK_WIDTH, DA_WIDTH,
            HB_WIDTH, HB_WIDTH, HB_WIDTH, HB_WIDTH, HB_WIDTH,
            GC_KEY_WIDTH, GC_KEY_WIDTH, GC_WIDTH, GC_WIDTH,
            GC_GATE_RANK, GC_GATE_RANK,
            N_BRANCH * D_MODEL)
D_IN = (2 * DA_QK_WIDTH + DA_WIDTH + 5 * HB_WIDTH + 2 * GC_KEY_WIDTH + 2 * GC_WIDTH
        + 2 * GC_GATE_RANK + N_BRANCH * D_MODEL)

kernel_name = "hybrid_diffattn_hgrn2_gla_encoder"


def rms_norm(x, g):
    xf = x.astype(jnp.float32)
    y = xf * lax.rsqrt(jnp.mean(xf * xf, axis=-1, keepdims=True) + EPS)
    return (y * g.astype(jnp.float32)).astype(x.dtype)


def split_cols(t, sizes):
    offs = np.cumsum(sizes)[:-1].tolist()
    return jnp.split(t, offs, axis=-1)


def to_heads(t, n_heads):
    b, s, w = t.shape
    return t.reshape(b, s, n_heads, w // n_heads).transpose(0, 2, 1, 3)


def from_heads(t):
    b, n, s, h = t.shape
    return t.transpose(0, 2, 1, 3).reshape(b, s, n * h)


def alibi_slopes(n):
    return jnp.array([2.0 ** (-8.0 * (i + 1) / n) for i in range(n)], dtype=jnp.float32)


def diff_attention(q, k, v, lam, slopes):
    b, h, _, s, d = q.shape
    nb = s // Q_BLOCK
    scale = d ** -0.5
    q_blocks = jnp.moveaxis(q.reshape(b, h, 2, nb, Q_BLOCK, d), 3, 0)
    starts = jnp.arange(nb, dtype=jnp.int32) * Q_BLOCK
    key_pos = jnp.arange(s, dtype=jnp.int32)

    def one_block(args):
        qb, start = args
        scores = jnp.einsum('bhiqd,bhikd->bhiqk', qb, k).astype(jnp.float32) * scale
        q_pos = start + jnp.arange(Q_BLOCK, dtype=jnp.int32)
        dist = jnp.abs(q_pos[:, None] - key_pos[None, :]).astype(jnp.float32)
        scores = scores - slopes[:, None, None, None] * dist
        p = jax.nn.softmax(scores, axis=-1)
        w = p[:, :, 0] - lam * p[:, :, 1]
        return jnp.einsum('bhqk,bhkv->bhqv', w.astype(v.dtype), v)

    out = lax.map(one_block, (q_blocks, starts))
    return jnp.moveaxis(out, 0, 2).reshape(b, h, s, v.shape[-1])


def chunk_scan(q, k, v, log_a):
    b, h, s, dk = q.shape
    dv = v.shape[-1]
    nc = s // CHUNK

    def to_chunks(t):
        return jnp.moveaxis(t.astype(jnp.float32).reshape(b, h, nc, CHUNK, t.shape[-1]), 2, 0)

    causal_in_chunk = jnp.tril(jnp.ones((CHUNK, CHUNK), dtype=bool))[:, :, None]

    def step(state, inp):
        qc, kc, vc, ac = inp
        cum = jnp.cumsum(ac, axis=2)
        o_inter = jnp.einsum('bhtk,bhkv->bhtv', qc * jnp.exp(cum), state)
        diff = cum[:, :, :, None, :] - cum[:, :, None, :, :]
        decay = jnp.where(causal_in_chunk, jnp.exp(jnp.where(causal_in_chunk, diff, 0.0)), 0.0)
        scores = jnp.einsum('bhtk,bhsk,bhtsk->bhts', qc, kc, decay)
        o_intra = jnp.einsum('bhts,bhsv->bhtv', scores, vc)
        last = cum[:, :, -1:, :]
        new_state = jnp.exp(last[:, :, 0, :])[..., None] * state + jnp.einsum(
            'bhsk,bhsv->bhkv', kc * jnp.exp(last - cum), vc)
        return new_state, o_inter + o_intra

    state0 = jnp.zeros((b, h, dk, dv), jnp.float32)
    _, o = lax.scan(step, state0, (to_chunks(q), to_chunks(k), to_chunks(v), to_chunks(log_a)))
    return jnp.moveaxis(o, 0, 2).reshape(b, h, s, dv)


def bidir_scan(q, k_fwd, k_bwd, v, la_fwd, la_bwd):
    flip = lambda t: jnp.flip(t, axis=2)
    fwd = chunk_scan(q, k_fwd, v, la_fwd)
    bwd = flip(chunk_scan(flip(q), flip(k_bwd), flip(v), flip(la_bwd)))
    return (fwd + bwd).astype(q.dtype)


def mixer_diff_attn(q_cols, k_cols, v_cols, lam_params, subln_g, layer_idx):
    b, s, _ = q_cols.shape

    def qk_heads(t):
        return t.reshape(b, s, DA_HEADS, 2, DA_HEAD_DIM).transpose(0, 2, 3, 1, 4)

    q = qk_heads(q_cols)
    k = qk_heads(k_cols)
    v = to_heads(v_cols, DA_HEADS)
    lam_init = 0.8 - 0.6 * math.exp(-0.3 * layer_idx)
    lp = lam_params.astype(jnp.float32)
    lam = jnp.exp(jnp.sum(lp[0] * lp[1])) - jnp.exp(jnp.sum(lp[2] * lp[3])) + lam_init
    o = diff_attention(q, k, v, lam, alibi_slopes(DA_HEADS))
    o = rms_norm(o, subln_g) * (1.0 - lam_init)
    return from_heads(o)


def mixer_hgrn2(q_cols, f_fwd_cols, f_bwd_cols, i_cols, g_cols, lower_bound, norm_g):
    q = jax.nn.silu(to_heads(q_cols, HB_HEADS)) * HB_HEAD_DIM ** -0.5
    v = to_heads(i_cols, HB_HEADS)

    def forget(cols, lb):
        z = to_heads(cols, HB_HEADS).astype(jnp.float32)
        lbh = lb.reshape(HB_HEADS, 1, HB_HEAD_DIM)
        log_f = jax.nn.log_sigmoid(z) + jnp.log1p(lbh * jnp.exp(jnp.minimum(-z, MAX_NEG_LOGIT)))
        k = (1.0 - lbh) * jax.nn.sigmoid(-z)
        return k, log_f

    k_f, la_f = forget(f_fwd_cols, lower_bound[0])
    k_b, la_b = forget(f_bwd_cols, lower_bound[1])
    o = rms_norm(bidir_scan(q, k_f, k_b, v, la_f, la_b), norm_g)
    return from_heads(o) * jax.nn.silu(g_cols)


def mixer_gla(q_cols, k_cols, v_cols, g_cols, lr_fwd, lr_bwd, gate_w2, gate_b, norm_g):
    q = to_heads(q_cols, GC_HEADS) * GC_KEY_DIM ** -0.5
    k = to_heads(k_cols, GC_HEADS)
    v = to_heads(v_cols, GC_HEADS)

    def log_decay(lr, w2, bias):
        z = (jnp.einsum('bsr,rk->bsk', lr, w2) + bias).astype(jnp.float32)
        return to_heads(jax.nn.log_sigmoid(z) / GC_GATE_NORMALIZER, GC_HEADS)

    la_f = log_decay(lr_fwd, gate_w2[0], gate_b[0])
    la_b = log_decay(lr_bwd, gate_w2[1], gate_b[1])
    o = rms_norm(bidir_scan(q, k, k, v, la_f, la_b), norm_g)
    return from_heads(o) * jax.nn.silu(g_cols)


def setup_inputs(seed: int = 0) -> dict:
    key = jax.random.key(seed)
    ks = jax.random.split(key, 21)
    nrm = lambda k, shape, scale: jax.random.normal(k, shape, jnp.float32) * scale
    gain = lambda k, shape: 1.0 + 0.05 * jax.random.normal(k, shape, jnp.float32)
    return {
        "x": nrm(ks[0], (BATCH, SEQ, D_MODEL), 1.0),
        "c": nrm(ks[1], (BATCH, D_MODEL), 1.0),
        "ada_w": nrm(ks[2], (DEPTH, D_MODEL, N_MOD * D_MODEL), 0.5 * D_MODEL ** -0.5),
        "ada_b": nrm(ks[3], (DEPTH, N_MOD * D_MODEL), 0.02),
        "norm_mix_g": gain(ks[4], (DEPTH, D_MODEL)),
        "norm_mlp_g": gain(ks[5], (DEPTH, D_MODEL)),
        "w_in": nrm(ks[6], (DEPTH, D_MODEL, D_IN), D_MODEL ** -0.5),
        "diff_lambda": nrm(ks[7], (DEPTH, 4, DA_HEAD_DIM), 0.1),
        "diff_subln_g": gain(ks[8], (DEPTH, DA_V_DIM)),
        "hgrn_lb_logits": nrm(ks[9], (DEPTH, 2, HB_WIDTH), 0.1),
        "hgrn_norm_g": gain(ks[10], (DEPTH, HB_HEAD_DIM)),
        "gla_gate_w2": nrm(ks[11], (DEPTH, 2, GC_GATE_RANK, GC_KEY_WIDTH), GC_GATE_RANK ** -0.5),
        "gla_gate_b": nrm(ks[12], (DEPTH, 2, GC_KEY_WIDTH), 0.1),
        "gla_norm_g": gain(ks[13], (DEPTH, GC_VAL_DIM)),
        "w_up_a": nrm(ks[14], (DEPTH, DA_WIDTH, D_MODEL), DA_WIDTH ** -0.5),
        "w_up_b": nrm(ks[15], (DEPTH, HB_WIDTH, D_MODEL), HB_WIDTH ** -0.5),
        "w_up_c": nrm(ks[16], (DEPTH, GC_WIDTH, D_MODEL), GC_WIDTH ** -0.5),
        "w_out": nrm(ks[17], (DEPTH, D_MODEL, D_MODEL), D_MODEL ** -0.5),
        "mlp_w1": nrm(ks[18], (DEPTH, D_MODEL, D_FF), D_MODEL ** -0.5),
        "mlp_w2": nrm(ks[19], (DEPTH, D_FF, D_MODEL), D_FF ** -0.5),
        "final_norm_g": gain(ks[20], (D_MODEL,)),
    }


def reference(x, c, ada_w, ada_b, norm_mix_g, norm_mlp_g, w_in, diff_lambda, diff_subln_g,
              hgrn_lb_logits, hgrn_norm_g, gla_gate_w2, gla_gate_b, gla_norm_g,
              w_up_a, w_up_b, w_up_c, w_out, mlp_w1, mlp_w2, final_norm_g):
    lb_w = jax.nn.softmax(hgrn_lb_logits.astype(jnp.float32), axis=0)
    lower_bounds = jnp.cumsum(lb_w, axis=0) - lb_w[0:1]
    cond = jax.nn.silu(c)
    for l in range(DEPTH):
        mod = cond @ ada_w[l] + ada_b[l]
        sh_m, sc_m, gt_m, sh_f, sc_f, gt_f = [t[:, None, :] for t in jnp.split(mod, N_MOD, axis=-1)]

        h = rms_norm(x, norm_mix_g[l]) * (1.0 + sc_m) + sh_m
        proj = jnp.einsum('bsd,de->bse', h, w_in[l])
        (a_q, a_k, a_v, b_q, b_ff, b_fb, b_i, b_g,
         c_q, c_k, c_v, c_g, c_lrf, c_lrb, gates) = split_cols(proj, IN_SIZES)
        o_a = mixer_diff_attn(a_q, a_k, a_v, diff_lambda[l], diff_subln_g[l], l)
        o_b = mixer_hgrn2(b_q, b_ff, b_fb, b_i, b_g, lower_bounds[l], hgrn_norm_g[l])
        o_c = mixer_gla(c_q, c_k, c_v, c_g, c_lrf, c_lrb, gla_gate_w2[l], gla_gate_b[l], gla_norm_g[l])
        g_a, g_b, g_c = jnp.split(jax.nn.sigmoid(gates), N_BRANCH, axis=-1)
        merged = g_a * (o_a @ w_up_a[l]) + g_b * (o_b @ w_up_b[l]) + g_c * (o_c @ w_up_c[l])
        x = x + gt_m * (merged @ w_out[l])

        h = rms_norm(x, norm_mlp_g[l]) * (1.0 + sc_f) + sh_f
        x = x + gt_f * (jnp.square(jax.nn.relu(h @ mlp_w1[l])) @ mlp_w2[l])
    return rms_norm(x, final_norm_g)
```

```python
import contextlib
import math
import numpy as np
import concourse.bass as bass
import concourse.mybir as mybir
from concourse.bass_utils import run_bass_kernel_spmd

F32 = mybir.dt.float32
BF16 = mybir.dt.bfloat16
ALU = mybir.AluOpType
AF = mybir.ActivationFunctionType
AX = mybir.AxisListType

ENGS = ("pe", "act", "dve", "pool", "sp")
NDSEM = 6

D = 1024
S = 2048
NT = 16
NB = 4
D_IN = 6688
EPS = 1e-6
C_AQ, C_AK, C_AV = 0, 512, 1024
C_BQ, C_BFF, C_BFB, C_BI, C_BG = 1536, 1792, 2048, 2304, 2560
C_CQ, C_CK, C_CV, C_CG = 2816, 2944, 3072, 3328
C_LRF, C_LRB = 3584, 3600
C_GATE = 3616
SLOPES = [2.0 ** (-8.0 * (i + 1) / 4) for i in range(4)]
STRIP_W = 3968
DOFF = 1920


class Op:
    __slots__ = ("eng", "fn", "deps", "marked", "ticket", "dsem", "dval", "is_dma")

    def __init__(self, eng, fn, is_dma=False):
        self.eng = eng
        self.fn = fn
        self.deps = []
        self.marked = False
        self.ticket = None
        self.dsem = None
        self.dval = None
        self.is_dma = is_dma


class Prog:
    def __init__(self, nc):
        self.nc = nc
        self.ops = {e: [] for e in ENGS}
        self.lastw = {}
        self.readers = {}
        self.dcount = {}
        self.drr = {e: 0 for e in ENGS}
        self.nops = 0
        self.fence = []
        self.lastdma = {}

    def barrier(self):
        f = []
        for e in ENGS:
            for op in reversed(self.ops[e]):
                if not op.is_dma:
                    f.append(op)
                    break
        f.extend(self.lastdma.values())
        for op in f:
            op.marked = True
        self.fence = f

    def _track(self, op, r, w, war):
        deps = list(self.fence)
        for k in r:
            lw = self.lastw.get(k)
            if lw is not None:
                deps.append(lw)
        for k in tuple(w) + tuple(war):
            lw = self.lastw.get(k)
            if lw is not None and (lw.is_dma or op.is_dma or lw.eng != op.eng):
                deps.append(lw)
            for rd in self.readers.get(k, ()):
                if rd.is_dma or op.is_dma or rd.eng != op.eng:
                    deps.append(rd)
        seen = set()
        for d in deps:
            if d is op or id(d) in seen:
                continue
            seen.add(id(d))
            op.deps.append(d)
            d.marked = True
        for k in r:
            if k in w:
                continue
            lst = self.readers.setdefault(k, [])
            if not op.is_dma:
                lst[:] = [x for x in lst if x.is_dma or x.eng != op.eng]
            lst.append(op)
        for k in w:
            self.lastw[k] = op
            self.readers[k] = []

    LIMIT = None
    TRACE = None

    def add(self, eng, fn, r=(), w=(), war=()):
        if Prog.LIMIT is not None and self.nops >= Prog.LIMIT:
            return None
        op = Op(eng, fn)
        if Prog.TRACE is not None:
            import sys
            f = sys._getframe(1)
            if f.f_code.co_name in ("mm", "tr", "act", "tt", "ts", "stt", "cp"):
                f = f.f_back
            Prog.TRACE.append((self.nops, eng, f.f_lineno))
        self._track(op, tuple(r), tuple(w), tuple(war))
        self.ops[eng].append(op)
        self.nops += 1
        return op

    def dma(self, q, out_ap, in_ap, r=(), w=(), war=()):
        if Prog.LIMIT is not None and self.nops >= Prog.LIMIT:
            return None
        def fn(e, out_ap=out_ap, in_ap=in_ap):
            return e.dma_start(out=out_ap, in_=in_ap)
        op = Op(q, fn, is_dma=True)
        k = self.drr[q]
        self.drr[q] = (k + 1) % NDSEM
        c = self.dcount.get((q, k), 0) + 1
        self.dcount[(q, k)] = c
        op.dsem = (q, k)
        op.dval = 16 * c
        op.marked = True
        self.lastdma[(q, k)] = op
        self._track(op, tuple(r), tuple(w), tuple(war))
        self.ops[q].append(op)
        self.nops += 1
        return op

    def emit(self, final_wait_ops=()):
        nc = self.nc
        with contextlib.ExitStack() as st:
            esem = {e: st.enter_context(nc.semaphore("s_" + e)) for e in ENGS}
            dsem = {}
            for q in ("sp", "act", "pool"):
                for k in range(NDSEM):
                    dsem[(q, k)] = st.enter_context(nc.semaphore("d_%s%d" % (q, k)))
            for e in ENGS:
                t = 0
                for op in self.ops[e]:
                    if op.is_dma:
                        continue
                    if op.marked:
                        t += 1
                        op.ticket = t
            block = st.enter_context(nc.Block())

            def run(ename, eng):
                seen = {}

                def need(sem, val):
                    key = id(sem)
                    if seen.get(key, 0) >= val:
                        return
                    seen[key] = val
                    eng.wait_ge(sem, val)

                for op in self.ops[ename]:
                    for d in op.deps:
                        if d.is_dma:
                            need(dsem[d.dsem], d.dval)
                        else:
                            need(esem[d.eng], d.ticket)
                    if op.is_dma:
                        if op.dval > 16:
                            need(dsem[op.dsem], op.dval - 16)
                        ins = op.fn(eng)
                        ins.then_inc(dsem[op.dsem], 16)
                    else:
                        ins = op.fn(eng)
                        if op.marked:
                            ins.then_inc(esem[ename], 1)
                if ename == "sp":
                    for d in final_wait_ops:
                        if d.is_dma:
                            need(dsem[d.dsem], d.dval)
                        else:
                            need(esem[d.eng], d.ticket)

            @block.tensor
            def _(e):
                run("pe", e)

            @block.scalar
            def _(e):
                run("act", e)

            @block.vector
            def _(e):
                run("dve", e)

            @block.gpsimd
            def _(e):
                run("pool", e)

            @block.sync
            def _(e):
                run("sp", e)


def lam_init_of(l):
    return 0.8 - 0.6 * math.exp(-0.3 * l)


def attn_tiles(h, J):
    out = []
    for kt in range(NT):
        o = 512 * J - 128 * kt
        lo, hi = o - 127, o + 511
        md = 0 if lo <= 0 <= hi else min(abs(lo), abs(hi))
        if SLOPES[h] * md <= 60.0:
            out.append(kt)
    return out


def build_program(NSEQ, DEPTH, dbg=0, PH=("attn", "scan", "merge", "mlp")):
    nc = bass.Bass("TRN2", target_bir_lowering=False)
    P = Prog(nc)

    def din(name, shape, dt=F32):
        return nc.dram_tensor(name, list(shape), dt, kind="ExternalInput").ap()

    x_d = din("x", [NSEQ, S, D])
    cT_d = din("cT", [128, 8 * NSEQ])
    adaw_d = din("ada_w", [DEPTH, D, 6 * D])
    adab_d = din("ada_bT", [128, DEPTH * 48])
    win_d = din("w_in", [DEPTH, D, D_IN])
    wup_d = din("w_up", [DEPTH, D, D])
    wout_d = din("w_out", [DEPTH, D, D])
    w1_d = din("mlp_w1", [DEPTH, D, 4 * D])
    w2_d = din("mlp_w2", [DEPTH, 4 * D, D])
    gmix_d = din("gmixT", [128, DEPTH * 8])
    gmlp_d = din("gmlpT", [128, DEPTH * 8])
    gfin_d = din("gfinT", [128, 8])
    subln_d = din("sublnT", [128, DEPTH])
    lamb_d = din("lamb", [128, DEPTH * 256])
    lbT_d = din("lbT", [128, DEPTH * 4])
    hg_d = din("hgB", [128, DEPTH * 64])
    gg_d = din("ggB", [128, DEPTH * 64])
    gb_d = din("gbT", [128, DEPTH * 4])
    gw2_d = din("gw2", [16, DEPTH * 2 * 128])
    strip_d = din("strips", [4, 128, STRIP_W])
    cm_d = din("cmasks", [128, 1408])
    out_d = nc.dram_tensor("out", [NSEQ, S, D], F32, kind="ExternalOutput").ap()
    if dbg:
        dbg_d = nc.dram_tensor("dbg", [128, 8, S], F32, kind="ExternalOutput").ap()

    with contextlib.ExitStack() as st:
        def sb(name, shape, dt):
            return st.enter_context(nc.sbuf_tensor(name, list(shape), dt))

        xT = sb("xT", [128, 8, S], F32)
        hT = sb("hT", [128, 8, S], BF16)
        oT = sb("oT", [128, 8, S], BF16)
        wsl = [sb("wsl%d" % i, [128, 4096], BF16) for i in range(2)]
        ARW = 12800
        arena = sb("arena", [128, ARW], F32)
        cmf = sb("cmf", [128, 384], F32)
        identb = sb("identb", [128, 128], BF16)
        mk2 = [sb("mk2_%d" % i, [128, 256], F32) for i in range(2)]
        rstb = sb("rstb", [128, 512], BF16)
        onesb = sb("onesb", [128, 128], BF16)
        modT = sb("modT", [128, DEPTH * 48 * NSEQ], F32)
        adab = sb("adab", [128, DEPTH * 48], F32)
        gmix = sb("gmix", [128, DEPTH * 8], F32)
        gmlp = sb("gmlp", [128, DEPTH * 8], F32)
        gfin = sb("gfin", [128, 8], F32)
        subln = sb("subln", [128, DEPTH], F32)
        neglam = sb("neglam", [128, DEPTH], F32)
        lbv = sb("lbv", [128, DEPTH * 4], F32)
        omlv = sb("omlv", [128, DEPTH * 4], F32)
        hgB = sb("hgB_s", [128, DEPTH * 64], F32)
        ggB = sb("ggB_s", [128, DEPTH * 64], F32)
        ngb = sb("ngb", [128, DEPTH * 4], F32)
        gw2 = sb("gw2_s", [16, DEPTH * 2 * 128], BF16)
        condT = sb("condT", [128, 8 * NSEQ], BF16)
        ab = sb("ab", [128, 16], F32)
        small = sb("small", [128, 64], F32)
        epsc = sb("epsc", [128, 2], F32)
        ps = [st.enter_context(nc.psum_tensor("ps%d" % i, [128, 512], F32)) for i in range(8)]

        identf = cmf[:, 0:128]
        BDh = cmf[:, 128:256]
        BDg = cmf[:, 256:384]

        def PSK(i):
            return ("ps", i)

        def A32(off, n):
            return arena[:, off:off + n]

        def A16(off32, n16):
            return arena[:, off32:off32 + n16 // 2].bitcast(BF16)

        def mm(out, lhsT, rhs, start, stop, r, w, skip=False):
            if skip:
                fn = lambda e: e.matmul(out, lhsT=lhsT, rhs=rhs, start=start, stop=stop, skip_group_check=True)
            else:
                fn = lambda e: e.matmul(out, lhsT=lhsT, rhs=rhs, start=start, stop=stop)
            return P.add("pe", fn, r=r, w=w)

        def tr(out, in_, ident, r, w):
            return P.add("pe", lambda e: e.transpose(out=out, in_=in_, identity=ident), r=r, w=w)

        def act(out, in_, func, r, w, bias=None, scale=1.0, eng="act"):
            if bias is None:
                fn = lambda e: e.activation(out=out, in_=in_, func=func, scale=scale)
            else:
                fn = lambda e: e.activation(out=out, in_=in_, func=func, bias=bias, scale=scale)
            return P.add("act", fn, r=r, w=w)

        def tt(out, in0, in1, op, r, w, eng="dve"):
            return P.add(eng, lambda e: e.tensor_tensor(out=out, in0=in0, in1=in1, op=op), r=r, w=w)

        def ts(out, in0, s1, s2, op0, op1, r, w, eng="dve"):
            if op1 is None:
                fn = lambda e: e.tensor_scalar(out=out, in0=in0, scalar1=s1, scalar2=None, op0=op0)
            else:
                fn = lambda e: e.tensor_scalar(out=out, in0=in0, scalar1=s1, scalar2=s2, op0=op0, op1=op1)
            return P.add(eng, fn, r=r, w=w)

        def stt(out, in0, scalar, in1, op0, op1, r, w, eng="dve"):
            return P.add(eng, lambda e: e.scalar_tensor_tensor(out=out, in0=in0, scalar=scalar, in1=in1, op0=op0, op1=op1), r=r, w=w)

        def cp(out, in_, r, w, eng="dve"):
            if eng == "act":
                return P.add("act", lambda e: e.copy(out=out, in_=in_), r=r, w=w)
            return P.add(eng, lambda e: e.tensor_copy(out=out, in_=in_), r=r, w=w)

        wstate = {"n": 0}

        def wload(pieces):
            s = wstate["n"] % 2
            wstate["n"] += 1
            for i, (dst, src) in enumerate(pieces):
                P.dma("pool", dst(wsl[s]), src, w=[("w", s, i)], war=[("w", s)])
            return s

        def wkeys(s, i=0):
            return [("w", s, i), ("w", s)]

        def kview(slot, off, kcn, n):
            return slot[:, off:off + kcn * n].rearrange("p (k c) -> p k c", c=n)

        def wsrc(ap2d):
            return ap2d.rearrange("(k p) c -> p k c", p=128)

        P.dma("sp", cmf[:, 0:128], cm_d[:, 0:128], w=["cmf"])
        P.dma("sp", cmf[:, 128:384], cm_d[:, 384:640], w=["cmf"])
        cmt = arena[:, 8000:8000 + 1408]
        P.dma("sp", cmt, cm_d, w=["cmt"])
        for (t_, d_) in ((adab, adab_d), (gmix, gmix_d), (gmlp, gmlp_d), (gfin, gfin_d), (subln, subln_d),
                         (hgB, hg_d), (ggB, gg_d)):
            P.dma("sp", t_[:], d_, w=["tables"])
        P.dma("pool", gw2[:], gw2_d, w=["gw2"])
        cp(identb[:], cmt[:, 0:128], r=["cmt"], w=["identb"])
        cp(rstb[:], cmt[:, 640:1152], r=["cmt"], w=["rstb"])
        for i_ in range(2):
            for hh_ in range(2):
                cp(mk2[i_][:, hh_ * 128:(hh_ + 1) * 128], cmt[:, 128 + 128 * i_:256 + 128 * i_], r=["cmt"], w=["mk2"])
        cp(onesb[:], cmt[:, 1152:1280], r=["cmt"], w=["onesb"])
        P.add("pool", lambda e: e.memset(epsc[:, 0:1], EPS), w=["epsc"])
        P.add("pool", lambda e: e.memset(epsc[:, 1:2], 1.0), w=["epsc"])
        eps_col = epsc[:, 0:1]
        one_col = epsc[:, 1:2]

        ctmp = A32(0, 8 * NSEQ)
        P.dma("sp", ctmp, cT_d, w=["ctmp"])
        act(condT[:], ctmp, AF.Silu, r=["ctmp"], w=["condT"])

        lamt = A32(64, DEPTH * 256)
        P.dma("sp", lamt, lamb_d, w=["lamt"])
        lprod = A32(64 + DEPTH * 256, DEPTH * 128)
        l3 = lamt.rearrange("p (l j d) -> p l j d", j=4, d=64)
        lp3 = lprod.rearrange("p (l j d) -> p l j d", j=2, d=64)
        tt(lp3[:, :, 0, :], l3[:, :, 0, :], l3[:, :, 1, :], ALU.mult, r=["lamt"], w=["lprod"])
        tt(lp3[:, :, 1, :], l3[:, :, 2, :], l3[:, :, 3, :], ALU.mult, r=["lamt", "lprod"], w=["lprod"])
        lsum = small[:, 0:DEPTH * 2]
        P.add("dve", lambda e: e.tensor_reduce(out=lsum, in_=lprod.rearrange("p (a d) -> p a d", d=64), axis=AX.X, op=ALU.add),
              r=["lprod"], w=["lsum"])
        lexp = small[:, 8:8 + DEPTH * 2]
        act(lexp, lsum, AF.Exp, r=["lsum"], w=["lexp"])
        le2 = lexp.rearrange("p (l j) -> p l j", j=2)
        tt(neglam[:], le2[:, :, 1], le2[:, :, 0], ALU.subtract, r=["lexp"], w=["neglam"])
        for l in range(DEPTH):
            ts(neglam[:, l:l + 1], neglam[:, l:l + 1], -lam_init_of(l), None, ALU.add, None, r=["neglam"], w=["neglam"])

        lbl = small[:, 16:16 + DEPTH * 4]
        P.dma("sp", lbl, lbT_d, w=["lbl"])
        lbe = small[:, 32:32 + DEPTH * 4]
        act(lbe, lbl, AF.Exp, r=["lbl"], w=["lbe"])
        lbs = small[:, 48:52]
        cp(lbs, lbe[:, 0:4], r=["lbe"], w=["lbs"])
        for l in range(1, DEPTH):
            tt(lbs, lbs, lbe[:, 4 * l:4 * l + 4], ALU.add, r=["lbs", "lbe"], w=["lbs"])
        lbr = small[:, 52:56]
        act(lbr, lbs, AF.Ln, r=["lbs"], w=["lbr"])
        act(lbr, lbr, AF.Exp, r=["lbr"], w=["lbr"], scale=-1.0)
        P.add("pool", lambda e: e.memset(lbv[:, 0:4], 0.0), w=["lbv"])
        for l in range(1, DEPTH):
            tt(lbe[:, 4 * l:4 * l + 4], lbe[:, 4 * l:4 * l + 4], lbr, ALU.mult, r=["lbe", "lbr"], w=["lbe"])
            tt(lbv[:, 4 * l:4 * l + 4], lbv[:, 4 * (l - 1):4 * l], lbe[:, 4 * l:4 * l + 4], ALU.add, r=["lbv", "lbe"], w=["lbv"])
        ts(omlv[:], lbv[:], -1.0, 1.0, ALU.mult, ALU.add, r=["lbv"], w=["omlv"])
        gbt = small[:, 56:56 + DEPTH * 4] if DEPTH * 4 <= 8 else A32(4000, DEPTH * 4)
        P.dma("sp", gbt, gb_d, w=["gbt"])
        ts(ngb[:], gbt, -1.0, None, ALU.mult, None, r=["gbt"], w=["ngb"])

        modv = modT[:].rearrange("p (l e s) -> p l e s", e=48, s=NSEQ)
        pi = 0
        for l in range(DEPTH):
            for cb in range(12):
                s_ = wload([(lambda sl: kview(sl, 0, 8, 512), wsrc(adaw_d[l, :, cb * 512:(cb + 1) * 512]))])
                wv = kview(wsl[s_], 0, 8, 512)
                for m in range(4):
                    ej = cb * 4 + m
                    bank = pi % 2
                    pi += 1
                    for kc in range(8):
                        mm(ps[bank][:, 0:NSEQ], wv[:, kc, m * 128:(m + 1) * 128],
                           condT[:, kc * NSEQ:(kc + 1) * NSEQ], kc == 0, kc == 7,
                           r=wkeys(s_) + ["condT"], w=[PSK(bank)])
                    ts(modv[:, l, ej, :], ps[bank][:, 0:NSEQ], adab[:, l * 48 + ej:l * 48 + ej + 1], None, ALU.add, None,
                       r=[PSK(bank), "tables"], w=["modT"])

        rot = {"p": 0}

        def nxt(lo, n):
            v = lo + rot["p"] % n
            rot["p"] += 1
            return v

        def modcol(l, j6, kc, s):
            return modv[:, l, j6 * 8 + kc, s:s + 1]

        def make_norm(l, s, gsrc, j_sc, j_sh):
            for kc in range(8):
                stt(ab[:, kc:kc + 1], modcol(l, j_sc, kc, s), 1.0, gsrc[:, l * 8 + kc:l * 8 + kc + 1], ALU.add, ALU.mult,
                    r=["modT", "tables"], w=["ab"])
            sq = A16(12800 - 2048, 4096).rearrange("p (k c) -> p k c", c=512)
            rstd = A32(12800 - 2048 - 512, 512)

            def p1(b):
                bs = slice(b * 512, (b + 1) * 512)
                act(sq, xT[:, :, bs], AF.Square, r=[("xT", b)], w=["n_sq"])

            def p2(b):
                bs = slice(b * 512, (b + 1) * 512)
                bank = nxt(0, 2)
                for kc in range(8):
                    mm(ps[bank][:], onesb[:], sq[:, kc, :], kc == 0, kc == 7, r=["onesb", "n_sq"], w=[PSK(bank)])
                act(rstd, ps[bank][:], AF.Ln, r=[PSK(bank), "epsc"], w=["n_rstd"], bias=eps_col, scale=1.0 / D)
                act(rstd, rstd, AF.Exp, r=["n_rstd"], w=["n_rstd"], scale=-0.5)
                for kc in range(8):
                    tmp = A32(12800 - 2048 - 512 - 1024 + 512 * (kc % 2), 512)
                    stt(tmp, xT[:, kc, bs], ab[:, kc:kc + 1], rstd, ALU.mult, ALU.mult,
                        r=[("xT", b), "ab", "n_rstd"], w=[("n_tmp", kc % 2)])
                    act(hT[:, kc, bs], tmp, AF.Identity, r=[("n_tmp", kc % 2), "modT"], w=[("hT", b)],
                        bias=modcol(l, j_sh, kc, s), scale=1.0)
            return p1, p2

        def do_norm(l, s, gsrc, j_sc, j_sh, tagr):
            p1, p2 = make_norm(l, s, gsrc, j_sc, j_sh)
            for b in range(NB):
                p1(b)
                p2(b)

        def norm_after_block(nrm, b):
            if nrm is None:
                return
            p1, p2 = nrm
            if b >= 1:
                p2(b - 1)
            p1(b)
            if b == NB - 1:
                p2(b)

        def proj_fm(l, col0, ncols, tokens, r_extra=()):
            raise NotImplementedError

        def load_x(s):
            for t in range(NT):
                buf = A32(1024 * (t % 2), 1024)
                P.dma("sp", buf, x_d[s, t * 128:(t + 1) * 128, :], w=[("xld", t % 2)])
                for g in range(2):
                    bank = nxt(0, 8)
                    for j in range(4):
                        kc = g * 4 + j
                        tr(ps[bank][:, j * 128:(j + 1) * 128], buf[:, kc * 128:(kc + 1) * 128], identf,
                           r=[("xld", t % 2), "cmf"], w=[PSK(bank)])
                    cp(xT[:, g * 4:(g + 1) * 4, t * 128:(t + 1) * 128],
                       ps[bank][:].rearrange("p (j c) -> p j c", c=128), r=[PSK(bank)], w=[("xT", t // 4)],
                       eng=("dve" if (t + g) % 2 == 0 else "act"))

        out_ops = []

        def store_out(s):
            sq = A16(12800 - 2048, 4096).rearrange("p (k c) -> p k c", c=512)
            rstd = A32(12800 - 2048 - 512, 512)
            for b in range(NB):
                bs = slice(b * 512, (b + 1) * 512)
                act(sq, xT[:, :, bs], AF.Square, r=[("xT", b)], w=["n_sq"])
                bank = nxt(0, 2)
                for kc in range(8):
                    mm(ps[bank][:], onesb[:], sq[:, kc, :], kc == 0, kc == 7, r=["onesb", "n_sq"], w=[PSK(bank)])
                act(rstd, ps[bank][:], AF.Ln, r=[PSK(bank), "epsc"], w=["n_rstd"], bias=eps_col, scale=1.0 / D)
                act(rstd, rstd, AF.Exp, r=["n_rstd"], w=["n_rstd"], scale=-0.5)
                yb = A32(0, 4096).rearrange("p (k c) -> p k c", c=512)
                for kc in range(8):
                    stt(yb[:, kc, :], xT[:, kc, bs], gfin[:, kc:kc + 1], rstd, ALU.mult, ALU.mult,
                        r=[("xT", b), "tables", "n_rstd"], w=[("yb", kc)])
                for tl in range(4):
                    t = b * 4 + tl
                    ob = A32(4096 + 1024 * (t % 2), 1024)
                    for g in range(2):
                        bank2 = nxt(2, 6)
                        for j in range(4):
                            kc = g * 4 + j
                            tr(ps[bank2][:, j * 128:(j + 1) * 128], yb[:, kc, tl * 128:(tl + 1) * 128], identf,
                               r=[("yb", kc), "cmf"], w=[PSK(bank2)])
                        cp(ob[:, g * 512:(g + 1) * 512], ps[bank2][:], r=[PSK(bank2)], w=[("ob", t % 2)],
                           eng=("dve" if g == 0 else "act"))
                    out_ops.append(P.dma("sp", out_d[s, t * 128:(t + 1) * 128, :], ob, r=[("ob", t % 2)]))

        def attention(l, s):
            strip = A16(0, STRIP_W)
            QT = [A16(1984, 2048), A16(8896, 2048)]
            KT = A16(3008, 2048)
            ALIAS = ["hid", ("rl", 0), ("rl", 1), ("n_tmp", 0), ("n_tmp", 1), "n_rstd", "n_sq"]
            P.add("dve", lambda e: e.memset(QT[0][64:128, :], 0.0), w=["QT"], war=ALIAS)
            P.add("dve", lambda e: e.memset(QT[1][0:64, :], 0.0), w=["QT"], war=ALIAS)
            V = A16(4032, 2048).rearrange("p (t c) -> p t c", c=128)
            Eb = [A16(5056 + 256 * i, 512) for i in range(2)] + [A16(9920 + 256 * i, 512) for i in range(2)]
            Pm = [A16(5568 + 256 * i, 512) for i in range(2)] + [A16(10432 + 256 * i, 512) for i in range(2)]
            SBK = (2, 3, 0, 1)
            SLA = 3
            nr = [A32(6080 + 512 * i, 512) for i in range(5)]
            sqb = A16(6080 + 512 * 5, 512)
            gcol = small[:, 60:61]
            ts(gcol, subln[:, l:l + 1], 1.0 - lam_init_of(l), None, ALU.mult, None, r=["tables"], w=["gcol"])
            for h in range(4):
                s_ = wload([
                    (lambda sl: kview(sl, 0, 8, 128), wsrc(win_d[l, :, C_AQ + h * 128:C_AQ + (h + 1) * 128])),
                    (lambda sl: kview(sl, 1024, 8, 128), wsrc(win_d[l, :, C_AK + h * 128:C_AK + (h + 1) * 128])),
                    (lambda sl: kview(sl, 2048, 8, 128), wsrc(win_d[l, :, C_AV + h * 128:C_AV + (h + 1) * 128])),
                ])
                wq = kview(wsl[s_], 0, 8, 128)
                wk = kview(wsl[s_], 1024, 8, 128)
                wv_ = kview(wsl[s_], 2048, 8, 128)
                P.dma("pool", strip, strip_d[h], w=["strip"], war=(ALIAS if h == 0 else ()))
                for b in range(NB):
                    bs = slice(b * 512, (b + 1) * 512)
                    bank = nxt(0, 2)
                    for kc in range(8):
                        mm(ps[bank][:], wq[:, kc, :], hT[:, kc, bs], kc == 0, kc == 7, r=wkeys(s_, 0) + [("hT", b)], w=[PSK(bank)])
                    cp(QT[0][0:64, bs], ps[bank][0:64, :], r=[PSK(bank)], w=["QT"], eng="act")
                    cp(QT[1][64:128, bs], ps[bank][64:128, :], r=[PSK(bank)], w=["QT"], eng="dve")
                    bank = nxt(0, 2)
                    for kc in range(8):
                        mm(ps[bank][:], wk[:, kc, :], hT[:, kc, bs], kc == 0, kc == 7, r=wkeys(s_, 1) + [("hT", b)], w=[PSK(bank)])
                    cp(KT[:, bs], ps[bank][:], r=[PSK(bank)], w=["KT"], eng="act")
                    bank = nxt(0, 2)
                    for tl in range(4):
                        t = b * 4 + tl
                        for kc in range(8):
                            mm(ps[bank][:, tl * 128:(tl + 1) * 128], hT[:, kc, t * 128:(t + 1) * 128], wv_[:, kc, :],
                               kc == 0, kc == 7, r=wkeys(s_, 2) + [("hT", b)], w=[PSK(bank)])
                    cp(V[:, b * 4:(b + 1) * 4, :], ps[bank][:].rearrange("p (t c) -> p t c", c=128), r=[PSK(bank)], w=["V"])
                for J in range(NB):
                    qs_ = slice(J * 512, (J + 1) * 512)
                    tiles = attn_tiles(h, J)
                    work = [(i, kt) for i in range(2) for kt in tiles]

                    def s_mm(n):
                        i, kt = work[n]
                        bk = SBK[n % 4]
                        mm(ps[bk][:], KT[:, kt * 128:(kt + 1) * 128], QT[i][:, qs_],
                           True, True, r=["KT", "QT"], w=[PSK(bk)])
                    for n0 in range(min(SLA, len(work))):
                        s_mm(n0)
                    for n, (i, kt) in enumerate(work):
                        if n + SLA < len(work):
                            s_mm(n + SLA)
                        bk = SBK[n % 4]
                        act(Eb[n % 4], ps[bk][:], AF.Exp, r=[PSK(bk)], w=[("E", n % 4)], scale=0.125)
                        off = 512 * J - 128 * kt + DOFF
                        tt(Pm[n % 4], Eb[n % 4], strip[:, off:off + 512], ALU.mult, r=[("E", n % 4), "strip"], w=[("Pm", n % 4)])
                        first = (kt == tiles[0])
                        last = (kt == tiles[-1])
                        mm(ps[4 + 2 * i][:], V[:, kt, :], Pm[n % 4], first, last, r=["V", ("Pm", n % 4)], w=[PSK(4 + 2 * i)])
                        mm(ps[5 + 2 * i][:], onesb[:], Pm[n % 4], first, last, r=["onesb", ("Pm", n % 4)], w=[PSK(5 + 2 * i)])
                    for i in range(2):
                        act(nr[i], ps[5 + 2 * i][:], AF.Ln, r=[PSK(5 + 2 * i)], w=[("nr", i)])
                        act(nr[i], nr[i], AF.Exp, r=[("nr", i)], w=[("nr", i)], scale=-1.0)
                        tt(nr[2 + i], ps[4 + 2 * i][:], nr[i], ALU.mult, r=[PSK(4 + 2 * i), ("nr", i)], w=[("nr", 2 + i)])
                    stt(nr[4], nr[3], neglam[:, l:l + 1], nr[2], ALU.mult, ALU.add, r=[("nr", 3), ("nr", 2), "neglam"], w=[("nr", 4)])
                    tt(sqb, nr[4], nr[4], ALU.mult, r=[("nr", 4)], w=["sqb"])
                    mm(ps[5][:], onesb[:], sqb, True, True, r=["onesb", "sqb"], w=[PSK(5)])
                    act(nr[0], ps[5][:], AF.Ln, r=[PSK(5), "epsc"], w=[("nr", 0)], bias=eps_col, scale=1.0 / 128)
                    act(nr[0], nr[0], AF.Exp, r=[("nr", 0)], w=[("nr", 0)], scale=-0.5)
                    stt(oT[:, h, qs_], nr[4], gcol, nr[0], ALU.mult, ALU.mult, r=[("nr", 4), ("nr", 0), "gcol"], w=["oT"])

        def scan_mixer(l, s, m, c):
            CP = 128 if m == 0 else 64
            dk = CP // 2
            qsT = A16(0, 2048)
            qpad = A16(1024, 4096).rearrange("p (t j c) -> p t j c", j=2, c=128)
            ktil = A16(3072, 2048)
            khat = A16(4096, 2048).rearrange("p (t c) -> p t c", c=128)
            V = A16(5120, 2048).rearrange("p (t c) -> p t c", c=128)
            oacc = A32(6144, 2048).rearrange("p (t c) -> p t c", c=128)
            dec = A32(8192, 32)
            emid = A32(8224, 40)
            Sst = A32(8320, 128)
            tmpU = A32(8448, 128)
            Sbf = A16(8576, 128)
            Am = [A16(8640 + 128 * i, 256) for i in range(2)]
            BT = [A32(8960 + 512 * i, 512) for i in range(7)]
            BD = BDh if m == 0 else BDg
            scale_e = 1.0 if m == 0 else -1.0 / 16.0
            if m == 0:
                cq, cv, cg = C_BQ + c * 128, C_BI + c * 128, C_BG + c * 128
                qscale = 0.125
            else:
                cq, cv, cg = C_CQ + c * 64, C_CV + c * 128, C_CG + c * 128
                qscale = 32 ** -0.5
            sA = wload([
                (lambda sl: kview(sl, 0, 8, CP), wsrc(win_d[l, :, cq:cq + CP])),
                (lambda sl: kview(sl, 1024, 8, 128), wsrc(win_d[l, :, cv:cv + 128])),
                (lambda sl: kview(sl, 2048, 8, 128), wsrc(win_d[l, :, cg:cg + 128])),
            ])
            wq = kview(wsl[sA], 0, 8, CP)
            wv_ = kview(wsl[sA], 1024, 8, 128)
            wg = kview(wsl[sA], 2048, 8, 128)
            if m == 0:
                sB = wload([
                    (lambda sl: kview(sl, 0, 8, 128), wsrc(win_d[l, :, C_BFF + c * 128:C_BFF + (c + 1) * 128])),
                    (lambda sl: kview(sl, 1024, 8, 128), wsrc(win_d[l, :, C_BFB + c * 128:C_BFB + (c + 1) * 128])),
                ])
                wz = [kview(wsl[sB], 0, 8, 128), kview(wsl[sB], 1024, 8, 128)]
                wkk = None
                wlr = None
            else:
                sB = wload([
                    (lambda sl: kview(sl, 0, 8, 64), wsrc(win_d[l, :, C_CK + c * 64:C_CK + (c + 1) * 64])),
                    (lambda sl: kview(sl, 1024, 8, 16), wsrc(win_d[l, :, C_LRF:C_LRF + 16])),
                    (lambda sl: kview(sl, 2048, 8, 16), wsrc(win_d[l, :, C_LRB:C_LRB + 16])),
                ])
                wkk = kview(wsl[sB], 0, 8, 64)
                wlr = [kview(wsl[sB], 1024, 8, 16), kview(wsl[sB], 2048, 8, 16)]
            for b in range(NB):
                bs = slice(b * 512, (b + 1) * 512)
                bank = nxt(0, 2)
                for kc in range(8):
                    mm(ps[bank][0:CP, :], wq[:, kc, :], hT[:, kc, bs], kc == 0, kc == 7, r=wkeys(sA, 0) + [("hT", b)], w=[PSK(bank)])
                if m == 0:
                    act(qsT[0:CP, bs], ps[bank][0:CP, :], AF.Silu, r=[PSK(bank)], w=["qsT"])
                else:
                    cp(qsT[0:CP, bs], ps[bank][0:CP, :], r=[PSK(bank)], w=["qsT"], eng="act")
                bank = nxt(0, 2)
                for tl in range(4):
                    t = b * 4 + tl
                    for kc in range(8):
                        mm(ps[bank][:, tl * 128:(tl + 1) * 128], hT[:, kc, t * 128:(t + 1) * 128], wv_[:, kc, :],
                           kc == 0, kc == 7, r=wkeys(sA, 1) + [("hT", b)], w=[PSK(bank)])
                cp(V[:, b * 4:(b + 1) * 4, :], ps[bank][:].rearrange("p (t c) -> p t c", c=128), r=[PSK(bank)], w=["sV"])
            P.add("dve", lambda e: e.memset(qpad[0:CP], 0.0), w=["qpad"])

            for dr in range(2):
                endp, midp = (63, 31) if dr == 0 else (0, 32)
                lidx = (l * 2 + dr) * 2 + c
                def prep_chain(b, hf):
                    NTK = 256
                    t0 = b * 512 + hf * NTK
                    hs = slice(t0, t0 + NTK)
                    fs = slice(hf * NTK, (hf + 1) * NTK)
                    la, Bc, d1, e1, e2, e3, kk = [x[:, fs] for x in BT]
                    K_ = lambda i: ("bt", i, hf)
                    bank = hf
                    nch = NTK // 64
                    if m == 0:
                        for kc in range(8):
                            mm(ps[bank][:, 0:NTK], wz[dr][:, kc, :], hT[:, kc, hs], kc == 0, kc == 7, r=wkeys(sB, dr) + [("hT", b)], w=[PSK(bank)])
                        yield
                        act(e1, ps[bank][:, 0:NTK], AF.Exp, r=[PSK(bank)], w=[K_(3)], scale=-1.0)
                        yield
                        act(e2, e1, AF.Ln, r=[K_(3), "epsc"], w=[K_(4)], bias=one_col, scale=1.0)
                        yield
                        act(e2, e2, AF.Exp, r=[K_(4)], w=[K_(4)], scale=-1.0)
                        yield
                        act(la, e2, AF.Ln, r=[K_(4), "lbv", "omlv"], w=[K_(0)],
                            bias=lbv[:, lidx:lidx + 1], scale=omlv[:, lidx:lidx + 1])
                        stt(kk, e1, omlv[:, lidx:lidx + 1], e2, ALU.mult, ALU.mult, r=[K_(3), K_(4), "omlv"], w=[K_(6)])
                        yield
                    else:
                        for kc in range(8):
                            mm(ps[bank][0:16, 0:NTK], wlr[dr][:, kc, :], hT[:, kc, hs], kc == 0, kc == 7, r=wkeys(sB, 1 + dr) + [("hT", b)], w=[PSK(bank)])
                        yield
                        lrT = d1[:, 0:NTK // 2].bitcast(BF16)
                        cp(lrT[0:16, :], ps[bank][0:16, 0:NTK], r=[PSK(bank)], w=[K_(2)])
                        yield
                        goff = (l * 2 + dr) * 128 + c * 64
                        mm(ps[bank][0:CP, 0:NTK], gw2[:, goff:goff + 64], lrT[0:16, :], True, True, r=["gw2", K_(2)], w=[PSK(bank)])
                        yield
                        act(e1[0:CP, :], ps[bank][0:CP, 0:NTK], AF.Exp, r=[PSK(bank), "ngb"], w=[K_(3)],
                            bias=ngb[0:CP, lidx:lidx + 1], scale=-1.0)
                        yield
                        act(la[0:CP, :], e1[0:CP, :], AF.Ln, r=[K_(3), "epsc"], w=[K_(0)], bias=one_col[0:CP], scale=1.0)
                        for kc in range(8):
                            mm(ps[bank][0:CP, 0:NTK], wkk[:, kc, :], hT[:, kc, hs], kc == 0, kc == 7, r=wkeys(sB, 0) + [("hT", b)], w=[PSK(bank)])
                        yield
                        cp(kk[0:CP, :], ps[bank][0:CP, 0:NTK], r=[PSK(bank)], w=[K_(6)], eng="act")
                        yield
                    P.add("dve", lambda e, Bc=Bc, la=la: e.tensor_tensor_scan(out=Bc[0:CP, :], data0=rstb[0:CP, 0:NTK], data1=la[0:CP, :],
                                                                             initial=0.0, op0=ALU.mult, op1=ALU.add),
                          r=["rstb", K_(0)], w=[K_(1)])
                    yield
                    B3 = Bc[0:CP, :].rearrange("p (c t) -> p c t", t=64)
                    la3 = la[0:CP, :].rearrange("p (c t) -> p c t", t=64)
                    d13 = d1[0:CP, :].rearrange("p (c t) -> p c t", t=64)
                    bkey, fkey, fbuf = K_(1), K_(0), la
                    if dr == 1:
                        tt(d13, la3, B3, ALU.subtract, r=[K_(0), K_(1)], w=[K_(2)])
                        yield
                        tt(la3, d13, B3[:, :, 63:64].to_broadcast([CP, nch, 64]), ALU.add,
                           r=[K_(2), K_(1)], w=[K_(0)])
                        yield
                        B3 = la3
                        bkey, fkey, fbuf = K_(0), K_(1), Bc
                    tt(d13, B3, B3[:, :, midp:midp + 1].to_broadcast([CP, nch, 64]), ALU.subtract, r=[bkey], w=[K_(2)])
                    yield
                    act(e1[0:CP, :], d1[0:CP, :], AF.Exp, r=[K_(2)], w=[K_(3)], scale=scale_e)
                    act(e2[0:CP, :], d1[0:CP, :], AF.Exp, r=[K_(2)], w=[K_(4)], scale=-scale_e)
                    yield
                    tt(d13, B3[:, :, endp:endp + 1].to_broadcast([CP, nch, 64]), B3, ALU.subtract, r=[bkey], w=[K_(2)])
                    yield
                    act(e3[0:CP, :], d1[0:CP, :], AF.Exp, r=[K_(2)], w=[K_(5)], scale=scale_e)
                    c0 = b * 8 + hf * nch
                    act(dec[0:CP, c0:c0 + nch], B3[:, :, endp], AF.Exp, r=[bkey], w=["dec"], scale=scale_e)
                    act(emid[0:CP, c0:c0 + nch], B3[:, :, midp], AF.Exp, r=[bkey], w=["emid"], scale=scale_e)
                    tl0 = b * 4 + hf * 2
                    qp = qpad[0:CP, tl0:tl0 + 2, :, :]
                    for j in range(2):
                        stt(qp[:, :, j, 64 * j:64 * j + 64], qsT[0:CP, hs].rearrange("p (t j c) -> p t j c", j=2, c=64)[:, :, j, :], qscale,
                            e1[0:CP, :].rearrange("p (t j c) -> p t j c", j=2, c=64)[:, :, j, :], ALU.mult, ALU.mult,
                            r=["qsT", K_(3)], w=["qpad"])
                    tt(ktil[0:CP, hs], kk[0:CP, :], e2[0:CP, :], ALU.mult, r=[K_(6), K_(4)], w=["ktil"])
                    yield
                    khT = fbuf[:, 0:NTK // 2].bitcast(BF16)
                    tt(khT[0:CP, :], kk[0:CP, :], e3[0:CP, :], ALU.mult, r=[K_(6), K_(5)], w=[fkey])
                    yield
                    pbf = ps[bank][:].bitcast(BF16)
                    for tl in range(2):
                        tr(pbf[:, tl * 128:tl * 128 + CP], khT[0:CP, tl * 128:(tl + 1) * 128], identb[0:CP, 0:CP],
                           r=[fkey, "identb"], w=[PSK(bank)])
                    yield
                    cp(khat[:, tl0:tl0 + 2, 0:CP], pbf[:, 0:256].rearrange("p (t c) -> p t c", c=128)[:, :, 0:CP],
                       r=[PSK(bank)], w=["khat"], eng="act")
                    yield

                H_ = (8 if m == 0 else 10) + dr
                gens = [prep_chain(b, hf) for b in range(NB) for hf in range(2)]
                done = [False] * len(gens)
                tstep = 0
                while not all(done):
                    for gi, g in enumerate(gens):
                        if done[gi] or tstep < gi * H_:
                            continue
                        try:
                            next(g)
                        except StopIteration:
                            done[gi] = True
                    tstep += 1
                P.add("dve", lambda e: e.memset(Sst[0:CP, :], 0.0), w=["Sst"])
                P.add("dve", lambda e: e.memset(Sbf[0:CP, :], 0.0), w=[("Sbf", 0)])
                for i_ in range(2):
                    P.add("dve", lambda e, i_=i_: e.memset(Am[i_], 0.0), w=[("Am", i_)])
                mki = mk2[dr][:].bitcast(mybir.dt.int32)
                torder = list(range(NT)) if dr == 0 else list(range(NT - 1, -1, -1))
                jorder = (0, 1) if dr == 0 else (1, 0)
                chunks = [(t, j) for t in torder for j in jorder]

                def a_mm(ti):
                    t = torder[ti]
                    bk = 2 + ti % 2
                    for hh in range(2):
                        rows = slice(dk * hh, dk * hh + dk)
                        for j in range(2):
                            rr = ["ktil", "qpad"] + ([PSK(bk)] if (hh == 1 and j == 0) else [])
                            mm(ps[bk][:, hh * 128 + 64 * j:hh * 128 + 64 * j + 64], ktil[rows, t * 128:(t + 1) * 128],
                               qpad[rows, t, j, 64 * j:64 * j + 64], True, True, r=rr, w=[PSK(bk)])

                def qplain(rows, t):
                    v = qpad[rows, t, :, :]
                    return v.rearrange("p j (a c) -> p (j a) c", a=2)[:, 0:4:3, :]

                R_ = 3
                LA_ = 2
                tmpUs = [A32(8448, 128), A32(12544, 128), A32(12672, 128)]
                Sbfs = [Sbf, A16(8896, 128)]
                P.add("dve", lambda e: e.memset(Sbfs[1][0:CP, :], 0.0), w=[("Sbf", 1)])

                def issue_U(ci):
                    t_, j_ = chunks[ci]
                    ub = 6 + ci % 2
                    mm(ps[ub][0:CP, 0:128], khat[64 * j_:64 * j_ + 64, t_, 0:CP], V[64 * j_:64 * j_ + 64, t_, :], True, True,
                       r=["khat", "sV"], w=[PSK(ub)])

                a_mm(0)
                for ci0 in range(LA_):
                    issue_U(ci0)
                for ti, t in enumerate(torder):
                    if ti + 1 < NT:
                        a_mm(ti + 1)
                    bk = 2 + ti % 2
                    am = Am[ti % 2]
                    P.add("dve", lambda e, am=am, bk=bk, mki=mki: e.copy_predicated(out=am, mask=mki, data=ps[bk][:, 0:256]),
                          r=[PSK(bk), "mk2", ("Am", ti % 2)], w=[("Am", ti % 2)])
                    ob = 4 + ti % 2
                    for ji, j in enumerate(jorder):
                        ci = ti * 2 + ji
                        cidx = t * 2 + j
                        sbf = Sbfs[ci % 2]
                        for hh in range(2):
                            rows = slice(dk * hh, dk * hh + dk)
                            rr = ["qpad", ("Sbf", ci % 2)] + ([PSK(ob)] if hh == 1 else [])
                            mm(ps[ob][:, hh * 64:(hh + 1) * 64], qpad[rows, t, j, :], sbf[rows, hh * 64:(hh + 1) * 64],
                               (ji == 0 and hh == 0), (ji == 1 and hh == 1), r=rr, w=[PSK(ob)], skip=True)
                        if ji == 0:
                            for hh in range(2):
                                mm(ps[ob][:, hh * 64:(hh + 1) * 64], am[:, hh * 128:(hh + 1) * 128], V[:, t, hh * 64:(hh + 1) * 64],
                                   False, False, r=[("Am", ti % 2), "sV"], w=[PSK(ob)], skip=True)
                        ub_ = 6 + ci % 2
                        stt(Sst[0:CP, :], Sst[0:CP, :], dec[0:CP, cidx:cidx + 1], ps[ub_][0:CP, 0:128], ALU.mult, ALU.add,
                            r=["Sst", "dec", PSK(ub_)], w=["Sst"])
                        if ci + LA_ < len(chunks):
                            issue_U(ci + LA_)
                        if ci + 1 < len(chunks):
                            tn, jn = chunks[ci + 1]
                            nidx = tn * 2 + jn
                            act(Sbfs[(ci + 1) % 2][0:CP, :], Sst[0:CP, :], AF.Copy, r=["Sst", "emid"], w=[("Sbf", (ci + 1) % 2)],
                                scale=emid[0:CP, nidx:nidx + 1])
                    if dr == 0:
                        cp(oacc[:, t, :], ps[ob][:, 0:128], r=[PSK(ob)], w=["oacc"], eng="act")
                    else:
                        tt(oacc[:, t, :], oacc[:, t, :], ps[ob][:, 0:128], ALU.add, r=[PSK(ob), "oacc"], w=["oacc"])
                P.barrier()
            P.barrier()
            gB = (hgB if m == 0 else ggB)[:, l * 64:(l + 1) * 64]
            osq = A32(8960, 2048)
            ss = A32(8960 + 2048, 32)
            on = A16(8960 + 2048 + 64, 2048)
            o2 = oacc.rearrange("p t c -> p (t c)")
            tt(osq, o2, o2, ALU.mult, r=["oacc"], w=["osq"])
            P.add("dve", lambda e: e.tensor_reduce(out=ss, in_=osq.rearrange("p (a d) -> p a d", d=64), axis=AX.X, op=ALU.add),
                  r=["osq"], w=["ss"])
            act(ss, ss, AF.Ln, r=["ss", "epsc"], w=["ss"], bias=eps_col, scale=1.0 / 64)
            act(ss, ss, AF.Exp, r=["ss"], w=["ss"], scale=-0.5)
            o3 = o2.rearrange("p (a d) -> p a d", d=64)
            tt(o3, o3, ss.unsqueeze(2).to_broadcast([128, 32, 64]), ALU.mult, r=["oacc", "ss"], w=["oacc"])
            tt(on.rearrange("p (a d) -> p a d", d=64), o3, gB.unsqueeze(1).to_broadcast([128, 32, 64]), ALU.mult,
               r=["oacc", "tables"], w=["on"])
            och = 4 + 2 * m + c
            for b in range(NB):
                bs = slice(b * 512, (b + 1) * 512)
                bank = nxt(0, 2)
                for kc in range(8):
                    mm(ps[bank][:], wg[:, kc, :], hT[:, kc, bs], kc == 0, kc == 7, r=wkeys(sA, 2) + [("hT", b)], w=[PSK(bank)])
                gT = A32(12288, 512)
                act(gT, ps[bank][:], AF.Silu, r=[PSK(bank)], w=["gT"])
                bank = nxt(0, 2)
                pbf = ps[bank][:].bitcast(BF16)
                for tl in range(4):
                    t = b * 4 + tl
                    tr(pbf[:, tl * 128:(tl + 1) * 128], on[:, t * 128:(t + 1) * 128], identb[:], r=["on", "identb"], w=[PSK(bank)])
                tt(oT[:, och, bs], pbf[:, 0:512], gT, ALU.mult, r=[PSK(bank), "gT"], w=["oT"])

        def merge_and_out(l, s, nrm_args=None):
            mT = A16(0, 16384).rearrange("p (k c) -> p k c", c=S)
            sg = [A32(8192 + 512 * i, 512) for i in range(3)]
            mt = [A32(8192 + 1536 + 512 * i, 512) for i in range(2)]
            ksl = (slice(0, 4), slice(4, 6), slice(6, 8))
            for ec in range(8):
                s_ = wload([
                    (lambda sl: kview(sl, 0, 8, 128), wsrc(win_d[l, :, C_GATE + ec * 128:C_GATE + (ec + 1) * 128])),
                    (lambda sl: kview(sl, 1024, 8, 128), wsrc(win_d[l, :, C_GATE + D + ec * 128:C_GATE + D + (ec + 1) * 128])),
                    (lambda sl: kview(sl, 2048, 8, 128), wsrc(win_d[l, :, C_GATE + 2 * D + ec * 128:C_GATE + 2 * D + (ec + 1) * 128])),
                    (lambda sl: kview(sl, 3072, 8, 128), wsrc(wup_d[l, :, ec * 128:(ec + 1) * 128])),
                ])
                wgt = [kview(wsl[s_], 1024 * i, 8, 128) for i in range(3)]
                wu = kview(wsl[s_], 3072, 8, 128)
                for b in range(NB):
                    bs = slice(b * 512, (b + 1) * 512)
                    for br in range(3):
                        for kc in range(8):
                            mm(ps[br][:], wgt[br][:, kc, :], hT[:, kc, bs], kc == 0, kc == 7, r=wkeys(s_, br) + [("hT", b)], w=[PSK(br)])
                        act(sg[br], ps[br][:], AF.Sigmoid, r=[PSK(br)], w=[("sg", br)])
                    for br in range(3):
                        kcs = list(range(8))[ksl[br]]
                        for kc in kcs:
                            mm(ps[3 + br][:], wu[:, kc, :], oT[:, kc, bs], kc == kcs[0], kc == kcs[-1], r=wkeys(s_, 3) + ["oT"], w=[PSK(3 + br)])
                    tt(mt[0], ps[3][:], sg[0], ALU.mult, r=[PSK(3), ("sg", 0)], w=[("mt", 0)])
                    tt(mt[1], ps[4][:], sg[1], ALU.mult, r=[PSK(4), ("sg", 1)], w=[("mt", 1)])
                    tt(mt[0], mt[0], mt[1], ALU.add, r=[("mt", 0), ("mt", 1)], w=[("mt", 0)])
                    tt(mt[1], ps[5][:], sg[2], ALU.mult, r=[PSK(5), ("sg", 2)], w=[("mt", 1)])
                    tt(mT[:, ec, bs], mt[0], mt[1], ALU.add, r=[("mt", 0), ("mt", 1)], w=["mT"])
            for eg in range(2):
                s_ = wload([(lambda sl: kview(sl, 0, 8, 512), wsrc(wout_d[l, :, eg * 512:(eg + 1) * 512]))])
                wo = kview(wsl[s_], 0, 8, 512)
                nrm = make_norm(*nrm_args) if (eg == 1 and nrm_args is not None) else None
                for b in range(NB):
                    bs = slice(b * 512, (b + 1) * 512)
                    for e4 in range(4):
                        ec = eg * 4 + e4
                        bank = nxt(0, 8)
                        for kc in range(8):
                            mm(ps[bank][:], wo[:, kc, e4 * 128:(e4 + 1) * 128], mT[:, kc, bs], kc == 0, kc == 7,
                               r=wkeys(s_) + ["mT"], w=[PSK(bank)])
                        stt(xT[:, ec, bs], ps[bank][:], modcol(l, 2, ec, s), xT[:, ec, bs], ALU.mult, ALU.add,
                            r=[PSK(bank), "modT", ("xT", b)], w=[("xT", b)])
                    norm_after_block(nrm, b)

        def mlp(l, s, nrm_args=None):
            hid = A16(0, 8192).rearrange("p (k c) -> p k c", c=S)
            rl = [A32(4096 + 512 * i, 512) for i in range(2)]
            for fg in range(8):
                s1 = wload([(lambda sl: kview(sl, 0, 8, 512), wsrc(w1_d[l, :, fg * 512:(fg + 1) * 512]))])
                w1v = kview(wsl[s1], 0, 8, 512)
                n = 0
                for fc in range(4):
                    for b in range(NB):
                        bs = slice(b * 512, (b + 1) * 512)
                        bank = nxt(0, 8)
                        for kc in range(8):
                            mm(ps[bank][:], w1v[:, kc, fc * 128:(fc + 1) * 128], hT[:, kc, bs], kc == 0, kc == 7,
                               r=wkeys(s1) + [("hT", b)], w=[PSK(bank)])
                        act(rl[n % 2], ps[bank][:], AF.Relu, r=[PSK(bank)], w=[("rl", n % 2)])
                        tt(hid[:, fc, bs], rl[n % 2], rl[n % 2], ALU.mult, r=[("rl", n % 2)], w=["hid"])
                        n += 1
                s2 = wload([(lambda sl: kview(sl, 0, 4, 1024), w2_d[l, fg * 512:(fg + 1) * 512, :].rearrange("(k p) c -> p k c", p=128))])
                w2v = kview(wsl[s2], 0, 4, 1024)
                nrm = make_norm(*nrm_args) if (fg == 7 and nrm_args is not None) else None
                for b in range(NB):
                    bs = slice(b * 512, (b + 1) * 512)
                    for ec in range(8):
                        bank = nxt(0, 8)
                        for fc in range(4):
                            mm(ps[bank][:], w2v[:, fc, ec * 128:(ec + 1) * 128], hid[:, fc, bs], fc == 0, fc == 3,
                               r=wkeys(s2) + ["hid"], w=[PSK(bank)])
                        stt(xT[:, ec, bs], ps[bank][:], modcol(l, 5, ec, s), xT[:, ec, bs], ALU.mult, ALU.add,
                            r=[PSK(bank), "modT", ("xT", b)], w=[("xT", b)])
                    norm_after_block(nrm, b)

        for s in range(NSEQ):
            P.barrier()
            load_x(s)
            for l in range(DEPTH):
                full = all(p in PH for p in ("attn", "scan", "merge", "mlp"))
                if l == 0:
                    P.barrier()
                    do_norm(l, s, gmix, 1, 0, "mix")
                elif not full:
                    do_norm(l, s, gmix, 1, 0, "mix")
                if "attn" in PH:
                    attention(l, s)
                for m in range(2):
                    for c in range(2):
                        if "scan" in PH:
                            P.barrier()
                            scan_mixer(l, s, m, c)
                P.barrier()
                if dbg == 1:
                    P.barrier()
                    for kc in range(8):
                        cp(xT[:, kc, :], oT[:, kc, :], r=["oT"], w=[("xT", 0), ("xT", 1), ("xT", 2), ("xT", 3)])
                if "merge" in PH:
                    merge_and_out(l, s, (l, s, gmlp, 4, 3) if full else None)
                if "mlp" in PH:
                    if not full:
                        do_norm(l, s, gmlp, 4, 3, "mlp")
                    mlp(l, s, (l + 1, s, gmix, 1, 0) if (full and l + 1 < DEPTH) else None)
            P.barrier()
            if dbg:
                out_ops.append(P.dma("sp", dbg_d, xT[:], r=[("xT", 0), ("xT", 1), ("xT", 2), ("xT", 3), "oT"]))
            store_out(s)
        P.emit(final_wait_ops=[o for o in out_ops if o is not None])
    return nc, P


def const_tables():
    cm = np.zeros((128, 1408), np.float32)
    cm[:, 0:128] = np.eye(128, dtype=np.float32)
    si = np.arange(128)[:, None]
    ti = np.arange(128)[None, :]
    same = (si // 64) == (ti // 64)
    cm[:, 128:256] = (same & (si <= ti)).astype(np.float32)
    cm[:, 256:384] = (same & (si >= ti)).astype(np.float32)
    cm[:, 384:512] = ((si // 64) == (ti // 64)).astype(np.float32)
    cm[:, 512:640] = ((si // 32) == (ti // 64)).astype(np.float32)
    rst = np.ones((128, 512), np.float32)
    rst[:, 0::64] = 0.0
    cm[:, 640:1152] = rst
    cm[:, 1152:1280] = 1.0
    strips = np.zeros((4, 128, STRIP_W), np.float32)
    p = np.arange(128, dtype=np.float64)[:, None]
    u = np.arange(STRIP_W, dtype=np.float64)[None, :]
    for h in range(4):
        strips[h] = np.exp(-SLOPES[h] * np.abs(u - p - DOFF)).astype(np.float32)
    return cm, strips


def colT(v, nchunk):
    lead = v.shape[:-1]
    a = v.reshape(lead + (nchunk, 128))
    a = np.moveaxis(a, -1, 0)
    return np.ascontiguousarray(a.reshape(128, -1))


def make_in_maps(inputs, ncores, nseq, depth):
    f = lambda a: np.ascontiguousarray(np.asarray(a, dtype=np.float32))
    x = f(inputs["x"])
    c = f(inputs["c"])
    cm, strips = const_tables()
    w_up = np.concatenate([f(inputs["w_up_a"])[:depth], f(inputs["w_up_b"])[:depth], f(inputs["w_up_c"])[:depth]], axis=1)
    gb = f(inputs["gla_gate_b"])[:depth]
    gbT = np.zeros((128, depth * 4), np.float32)
    gbT[:64] = np.moveaxis(gb.reshape(depth, 2, 2, 64), -1, 0).reshape(64, -1)
    shared = {
        "ada_w": f(inputs["ada_w"])[:depth],
        "ada_bT": colT(f(inputs["ada_b"])[:depth], 48),
        "w_in": f(inputs["w_in"])[:depth],
        "w_up": np.ascontiguousarray(w_up),
        "w_out": f(inputs["w_out"])[:depth],
        "mlp_w1": f(inputs["mlp_w1"])[:depth],
        "mlp_w2": f(inputs["mlp_w2"])[:depth],
        "gmixT": colT(f(inputs["norm_mix_g"])[:depth], 8),
        "gmlpT": colT(f(inputs["norm_mlp_g"])[:depth], 8),
        "gfinT": colT(f(inputs["final_norm_g"]), 8),
        "sublnT": colT(f(inputs["diff_subln_g"])[:depth], 1),
        "lamb": np.ascontiguousarray(np.broadcast_to(f(inputs["diff_lambda"])[:depth].reshape(1, -1), (128, depth * 256))),
        "lbT": colT(f(inputs["hgrn_lb_logits"])[:depth], 2),
        "hgB": np.ascontiguousarray(np.broadcast_to(f(inputs["hgrn_norm_g"])[:depth].reshape(1, -1), (128, depth * 64))),
        "ggB": np.ascontiguousarray(np.broadcast_to(f(inputs["gla_norm_g"])[:depth].reshape(1, -1), (128, depth * 64))),
        "gbT": gbT,
        "gw2": np.ascontiguousarray(np.moveaxis(f(inputs["gla_gate_w2"])[:depth], 2, 0).reshape(16, -1)),
        "strips": strips,
        "cmasks": cm,
    }
    maps = []
    for i in range(ncores):
        m = dict(shared)
        m["x"] = np.ascontiguousarray(x[i * nseq:(i + 1) * nseq])
        m["cT"] = colT(c[i * nseq:(i + 1) * nseq], 8)
        cc = c[i * nseq:(i + 1) * nseq].reshape(nseq, 8, 128)
        m["cT"] = np.ascontiguousarray(np.transpose(cc, (2, 1, 0)).reshape(128, 8 * nseq))
        maps.append(m)
    return maps


_CACHE = {}


def kernel(**inputs):
    ncores, nseq, depth = 8, 4, 4
    if "nc" not in _CACHE:
        _CACHE["nc"] = build_program(nseq, depth)[0]
    nc = _CACHE["nc"]
    maps = make_in_maps(inputs, ncores, nseq, depth)
    res = run_bass_kernel_spmd(nc, maps, core_ids=list(range(ncores)))
    out = np.concatenate([r["out"] for r in res.results], axis=0)
    return out.astype(np.float32)
```

```python
import contextlib
import math
import numpy as np
import concourse.bass as bass
import concourse.mybir as mybir
from concourse.bass_utils import run_bass_kernel_spmd

F32 = mybir.dt.float32
BF16 = mybir.dt.bfloat16
ALU = mybir.AluOpType
AF = mybir.ActivationFunctionType
AX = mybir.AxisListType

ENGS = ("pe", "act", "dve", "pool", "sp")
NDSEM = 6

D = 1024
S = 2048
NT = 16
NB = 4
D_IN = 6688
EPS = 1e-6
C_AQ, C_AK, C_AV = 0, 512, 1024
C_BQ, C_BFF, C_BFB, C_BI, C_BG = 1536, 1792, 2048, 2304, 2560
C_CQ, C_CK, C_CV, C_CG = 2816, 2944, 3072, 3328
C_LRF, C_LRB = 3584, 3600
C_GATE = 3616
SLOPES = [2.0 ** (-8.0 * (i + 1) / 4) for i in range(4)]
STRIP_W = 3968
DOFF = 1920


class Op:
    __slots__ = ("eng", "fn", "deps", "marked", "ticket", "dsem", "dval", "is_dma")

    def __init__(self, eng, fn, is_dma=False):
        self.eng = eng
        self.fn = fn
        self.deps = []
        self.marked = False
        self.ticket = None
        self.dsem = None
        self.dval = None
        self.is_dma = is_dma


class Prog:
    def __init__(self, nc):
        self.nc = nc
        self.ops = {e: [] for e in ENGS}
        self.lastw = {}
        self.readers = {}
        self.dcount = {}
        self.drr = {e: 0 for e in ENGS}
        self.nops = 0
        self.fence = []
        self.lastdma = {}

    def barrier(self):
        f = []
        for e in ENGS:
            for op in reversed(self.ops[e]):
                if not op.is_dma:
                    f.append(op)
                    break
        f.extend(self.lastdma.values())
        for op in f:
            op.marked = True
        self.fence = f

    def _track(self, op, r, w, war):
        deps = list(self.fence)
        for k in r:
            lw = self.lastw.get(k)
            if lw is not None:
                deps.append(lw)
        for k in tuple(w) + tuple(war):
            lw = self.lastw.get(k)
            if lw is not None and (lw.is_dma or op.is_dma or lw.eng != op.eng):
                deps.append(lw)
            for rd in self.readers.get(k, ()):
                if rd.is_dma or op.is_dma or rd.eng != op.eng:
                    deps.append(rd)
        seen = set()
        for d in deps:
            if d is op or id(d) in seen:
                continue
            seen.add(id(d))
            op.deps.append(d)
            d.marked = True
        for k in r:
            if k in w:
                continue
            lst = self.readers.setdefault(k, [])
            if not op.is_dma:
                lst[:] = [x for x in lst if x.is_dma or x.eng != op.eng]
            lst.append(op)
        for k in w:
            self.lastw[k] = op
            self.readers[k] = []

    LIMIT = None
    TRACE = None

    def add(self, eng, fn, r=(), w=(), war=()):
        if Prog.LIMIT is not None and self.nops >= Prog.LIMIT:
            return None
        op = Op(eng, fn)
        if Prog.TRACE is not None:
            import sys
            f = sys._getframe(1)
            if f.f_code.co_name in ("mm", "tr", "act", "tt", "ts", "stt", "cp"):
                f = f.f_back
            Prog.TRACE.append((self.nops, eng, f.f_lineno))
        self._track(op, tuple(r), tuple(w), tuple(war))
        self.ops[eng].append(op)
        self.nops += 1
        return op

    def dma(self, q, out_ap, in_ap, r=(), w=(), war=()):
        if Prog.LIMIT is not None and self.nops >= Prog.LIMIT:
            return None
        def fn(e, out_ap=out_ap, in_ap=in_ap):
            return e.dma_start(out=out_ap, in_=in_ap)
        op = Op(q, fn, is_dma=True)
        k = self.drr[q]
        self.drr[q] = (k + 1) % NDSEM
        c = self.dcount.get((q, k), 0) + 1
        self.dcount[(q, k)] = c
        op.dsem = (q, k)
        op.dval = 16 * c
        op.marked = True
        self.lastdma[(q, k)] = op
        self._track(op, tuple(r), tuple(w), tuple(war))
        self.ops[q].append(op)
        self.nops += 1
        return op

    def emit(self, final_wait_ops=()):
        nc = self.nc
        with contextlib.ExitStack() as st:
            esem = {e: st.enter_context(nc.semaphore("s_" + e)) for e in ENGS}
            dsem = {}
            for q in ("sp", "act", "pool"):
                for k in range(NDSEM):
                    dsem[(q, k)] = st.enter_context(nc.semaphore("d_%s%d" % (q, k)))
            for e in ENGS:
                t = 0
                for op in self.ops[e]:
                    if op.is_dma:
                        continue
                    if op.marked:
                        t += 1
                        op.ticket = t
            block = st.enter_context(nc.Block())

            def run(ename, eng):
                seen = {}

                def need(sem, val):
                    key = id(sem)
                    if seen.get(key, 0) >= val:
                        return
                    seen[key] = val
                    eng.wait_ge(sem, val)

                for op in self.ops[ename]:
                    for d in op.deps:
                        if d.is_dma:
                            need(dsem[d.dsem], d.dval)
                        else:
                            need(esem[d.eng], d.ticket)
                    if op.is_dma:
                        if op.dval > 16:
                            need(dsem[op.dsem], op.dval - 16)
                        ins = op.fn(eng)
                        ins.then_inc(dsem[op.dsem], 16)
                    else:
                        ins = op.fn(eng)
                        if op.marked:
                            ins.then_inc(esem[ename], 1)
                if ename == "sp":
                    for d in final_wait_ops:
                        if d.is_dma:
                            need(dsem[d.dsem], d.dval)
                        else:
                            need(esem[d.eng], d.ticket)

            @block.tensor
            def _(e):
                run("pe", e)

            @block.scalar
            def _(e):
                run("act", e)

            @block.vector
            def _(e):
                run("dve", e)

            @block.gpsimd
            def _(e):
                run("pool", e)

            @block.sync
            def _(e):
                run("sp", e)


def lam_init_of(l):
    return 0.8 - 0.6 * math.exp(-0.3 * l)


def attn_tiles(h, J):
    out = []
    for kt in range(NT):
        o = 512 * J - 128 * kt
        lo, hi = o - 127, o + 511
        md = 0 if lo <= 0 <= hi else min(abs(lo), abs(hi))
        if SLOPES[h] * md <= 44.0:
            out.append(kt)
    return out


def build_program(NSEQ, DEPTH, dbg=0, PH=("attn", "scan", "merge", "mlp")):
    nc = bass.Bass("TRN2", target_bir_lowering=False)
    P = Prog(nc)

    def din(name, shape, dt=F32):
        return nc.dram_tensor(name, list(shape), dt, kind="ExternalInput").ap()

    x_d = din("x", [NSEQ, S, D])
    cT_d = din("cT", [128, 8 * NSEQ])
    adaw_d = din("ada_w", [DEPTH, D, 6 * D])
    adab_d = din("ada_bT", [128, DEPTH * 48])
    win_d = din("w_in", [DEPTH, D, D_IN])
    wup_d = din("w_up", [DEPTH, D, D])
    wout_d = din("w_out", [DEPTH, D, D])
    w1_d = din("mlp_w1", [DEPTH, D, 4 * D])
    w2_d = din("mlp_w2", [DEPTH, 4 * D, D])
    gmix_d = din("gmixT", [128, DEPTH * 8])
    gmlp_d = din("gmlpT", [128, DEPTH * 8])
    gfin_d = din("gfinT", [128, 8])
    subln_d = din("sublnT", [128, DEPTH])
    lamb_d = din("lamb", [128, DEPTH * 256])
    lbT_d = din("lbT", [128, DEPTH * 4])
    hg_d = din("hgB", [128, DEPTH * 64])
    gg_d = din("ggB", [128, DEPTH * 64])
    gb_d = din("gbT", [128, DEPTH * 4])
    gw2_d = din("gw2", [16, DEPTH * 2 * 128])
    strip_d = din("strips", [4, 128, STRIP_W])
    cm_d = din("cmasks", [128, 1408])
    out_d = nc.dram_tensor("out", [NSEQ, S, D], F32, kind="ExternalOutput").ap()
    if dbg:
        dbg_d = nc.dram_tensor("dbg", [128, 8, S], F32, kind="ExternalOutput").ap()

    with contextlib.ExitStack() as st:
        def sb(name, shape, dt):
            return st.enter_context(nc.sbuf_tensor(name, list(shape), dt))

        xT = sb("xT", [128, 8, S], F32)
        hT = sb("hT", [128, 8, S], BF16)
        oT = sb("oT", [128, 8, S], BF16)
        wsl = [sb("wsl%d" % i, [128, 4096], BF16) for i in range(2)]
        ARW = 12800
        arena = sb("arena", [128, ARW], F32)
        cmf = sb("cmf", [128, 384], F32)
        identb = sb("identb", [128, 128], BF16)
        mk2 = [sb("mk2_%d" % i, [128, 256], F32) for i in range(2)]
        rstb = sb("rstb", [128, 512], BF16)
        onesb = sb("onesb", [128, 128], BF16)
        modT = sb("modT", [128, DEPTH * 48 * NSEQ], F32)
        adab = sb("adab", [128, DEPTH * 48], F32)
        gmix = sb("gmix", [128, DEPTH * 8], F32)
        gmlp = sb("gmlp", [128, DEPTH * 8], F32)
        gfin = sb("gfin", [128, 8], F32)
        subln = sb("subln", [128, DEPTH], F32)
        neglam = sb("neglam", [128, DEPTH], F32)
        lbv = sb("lbv", [128, DEPTH * 4], F32)
        omlv = sb("omlv", [128, DEPTH * 4], F32)
        hgB = sb("hgB_s", [128, DEPTH * 64], F32)
        ggB = sb("ggB_s", [128, DEPTH * 64], F32)
        ngb = sb("ngb", [128, DEPTH * 4], F32)
        gw2 = sb("gw2_s", [16, DEPTH * 2 * 128], BF16)
        condT = sb("condT", [128, 8 * NSEQ], BF16)
        ab = sb("ab", [128, 16], F32)
        small = sb("small", [128, 64], F32)
        epsc = sb("epsc", [128, 2], F32)
        ps = [st.enter_context(nc.psum_tensor("ps%d" % i, [128, 512], F32)) for i in range(8)]

        identf = cmf[:, 0:128]
        BDh = cmf[:, 128:256]
        BDg = cmf[:, 256:384]

        def PSK(i):
            return ("ps", i)

        def A32(off, n):
            return arena[:, off:off + n]

        def A16(off32, n16):
            return arena[:, off32:off32 + n16 // 2].bitcast(BF16)

        def mm(out, lhsT, rhs, start, stop, r, w, skip=False):
            if skip:
                fn = lambda e: e.matmul(out, lhsT=lhsT, rhs=rhs, start=start, stop=stop, skip_group_check=True)
            else:
                fn = lambda e: e.matmul(out, lhsT=lhsT, rhs=rhs, start=start, stop=stop)
            return P.add("pe", fn, r=r, w=w)

        def tr(out, in_, ident, r, w):
            return P.add("pe", lambda e: e.transpose(out=out, in_=in_, identity=ident), r=r, w=w)

        def act(out, in_, func, r, w, bias=None, scale=1.0, eng="act"):
            if bias is None:
                fn = lambda e: e.activation(out=out, in_=in_, func=func, scale=scale)
            else:
                fn = lambda e: e.activation(out=out, in_=in_, func=func, bias=bias, scale=scale)
            return P.add("act", fn, r=r, w=w)

        def tt(out, in0, in1, op, r, w, eng="dve"):
            return P.add(eng, lambda e: e.tensor_tensor(out=out, in0=in0, in1=in1, op=op), r=r, w=w)

        def ts(out, in0, s1, s2, op0, op1, r, w, eng="dve"):
            if op1 is None:
                fn = lambda e: e.tensor_scalar(out=out, in0=in0, scalar1=s1, scalar2=None, op0=op0)
            else:
                fn = lambda e: e.tensor_scalar(out=out, in0=in0, scalar1=s1, scalar2=s2, op0=op0, op1=op1)
            return P.add(eng, fn, r=r, w=w)

        def stt(out, in0, scalar, in1, op0, op1, r, w, eng="dve"):
            return P.add(eng, lambda e: e.scalar_tensor_tensor(out=out, in0=in0, scalar=scalar, in1=in1, op0=op0, op1=op1), r=r, w=w)

        def cp(out, in_, r, w, eng="dve"):
            if eng == "act":
                return P.add("act", lambda e: e.copy(out=out, in_=in_), r=r, w=w)
            return P.add(eng, lambda e: e.tensor_copy(out=out, in_=in_), r=r, w=w)

        wstate = {"n": 0}

        def wload(pieces):
            s = wstate["n"] % 2
            wstate["n"] += 1
            for i, (dst, src) in enumerate(pieces):
                P.dma("pool", dst(wsl[s]), src, w=[("w", s, i)], war=[("w", s)])
            return s

        def wkeys(s, i=0):
            return [("w", s, i), ("w", s)]

        def kview(slot, off, kcn, n):
            return slot[:, off:off + kcn * n].rearrange("p (k c) -> p k c", c=n)

        def wsrc(ap2d):
            return ap2d.rearrange("(k p) c -> p k c", p=128)

        P.dma("sp", cmf[:, 0:128], cm_d[:, 0:128], w=["cmf"])
        P.dma("sp", cmf[:, 128:384], cm_d[:, 384:640], w=["cmf"])
        cmt = arena[:, 8000:8000 + 1408]
        P.dma("sp", cmt, cm_d, w=["cmt"])
        for (t_, d_) in ((adab, adab_d), (gmix, gmix_d), (gmlp, gmlp_d), (gfin, gfin_d), (subln, subln_d),
                         (hgB, hg_d), (ggB, gg_d)):
            P.dma("sp", t_[:], d_, w=["tables"])
        P.dma("pool", gw2[:], gw2_d, w=["gw2"])
        cp(identb[:], cmt[:, 0:128], r=["cmt"], w=["identb"])
        cp(rstb[:], cmt[:, 640:1152], r=["cmt"], w=["rstb"])
        for i_ in range(2):
            for hh_ in range(2):
                cp(mk2[i_][:, hh_ * 128:(hh_ + 1) * 128], cmt[:, 128 + 128 * i_:256 + 128 * i_], r=["cmt"], w=["mk2"])
        cp(onesb[:], cmt[:, 1152:1280], r=["cmt"], w=["onesb"])
        P.add("pool", lambda e: e.memset(epsc[:, 0:1], EPS), w=["epsc"])
        P.add("pool", lambda e: e.memset(epsc[:, 1:2], 1.0), w=["epsc"])
        eps_col = epsc[:, 0:1]
        one_col = epsc[:, 1:2]

        ctmp = A32(0, 8 * NSEQ)
        P.dma("sp", ctmp, cT_d, w=["ctmp"])
        act(condT[:], ctmp, AF.Silu, r=["ctmp"], w=["condT"])

        lamt = A32(64, DEPTH * 256)
        P.dma("sp", lamt, lamb_d, w=["lamt"])
        lprod = A32(64 + DEPTH * 256, DEPTH * 128)
        l3 = lamt.rearrange("p (l j d) -> p l j d", j=4, d=64)
        lp3 = lprod.rearrange("p (l j d) -> p l j d", j=2, d=64)
        tt(lp3[:, :, 0, :], l3[:, :, 0, :], l3[:, :, 1, :], ALU.mult, r=["lamt"], w=["lprod"])
        tt(lp3[:, :, 1, :], l3[:, :, 2, :], l3[:, :, 3, :], ALU.mult, r=["lamt", "lprod"], w=["lprod"])
        lsum = small[:, 0:DEPTH * 2]
        P.add("dve", lambda e: e.tensor_reduce(out=lsum, in_=lprod.rearrange("p (a d) -> p a d", d=64), axis=AX.X, op=ALU.add),
              r=["lprod"], w=["lsum"])
        lexp = small[:, 8:8 + DEPTH * 2]
        act(lexp, lsum, AF.Exp, r=["lsum"], w=["lexp"])
        le2 = lexp.rearrange("p (l j) -> p l j", j=2)
        tt(neglam[:], le2[:, :, 1], le2[:, :, 0], ALU.subtract, r=["lexp"], w=["neglam"])
        for l in range(DEPTH):
            ts(neglam[:, l:l + 1], neglam[:, l:l + 1], -lam_init_of(l), None, ALU.add, None, r=["neglam"], w=["neglam"])

        lbl = small[:, 16:16 + DEPTH * 4]
        P.dma("sp", lbl, lbT_d, w=["lbl"])
        lbe = small[:, 32:32 + DEPTH * 4]
        act(lbe, lbl, AF.Exp, r=["lbl"], w=["lbe"])
        lbs = small[:, 48:52]
        cp(lbs, lbe[:, 0:4], r=["lbe"], w=["lbs"])
        for l in range(1, DEPTH):
            tt(lbs, lbs, lbe[:, 4 * l:4 * l + 4], ALU.add, r=["lbs", "lbe"], w=["lbs"])
        lbr = small[:, 52:56]
        act(lbr, lbs, AF.Ln, r=["lbs"], w=["lbr"])
        act(lbr, lbr, AF.Exp, r=["lbr"], w=["lbr"], scale=-1.0)
        P.add("pool", lambda e: e.memset(lbv[:, 0:4], 0.0), w=["lbv"])
        for l in range(1, DEPTH):
            tt(lbe[:, 4 * l:4 * l + 4], lbe[:, 4 * l:4 * l + 4], lbr, ALU.mult, r=["lbe", "lbr"], w=["lbe"])
            tt(lbv[:, 4 * l:4 * l + 4], lbv[:, 4 * (l - 1):4 * l], lbe[:, 4 * l:4 * l + 4], ALU.add, r=["lbv", "lbe"], w=["lbv"])
        ts(omlv[:], lbv[:], -1.0, 1.0, ALU.mult, ALU.add, r=["lbv"], w=["omlv"])
        gbt = small[:, 56:56 + DEPTH * 4] if DEPTH * 4 <= 8 else A32(4000, DEPTH * 4)
        P.dma("sp", gbt, gb_d, w=["gbt"])
        ts(ngb[:], gbt, -1.0, None, ALU.mult, None, r=["gbt"], w=["ngb"])

        modv = modT[:].rearrange("p (l e s) -> p l e s", e=48, s=NSEQ)
        pi = 0
        for l in range(DEPTH):
            for cb in range(12):
                s_ = wload([(lambda sl: kview(sl, 0, 8, 512), wsrc(adaw_d[l, :, cb * 512:(cb + 1) * 512]))])
                wv = kview(wsl[s_], 0, 8, 512)
                for m in range(4):
                    ej = cb * 4 + m
                    bank = pi % 2
                    pi += 1
                    for kc in range(8):
                        mm(ps[bank][:, 0:NSEQ], wv[:, kc, m * 128:(m + 1) * 128],
                           condT[:, kc * NSEQ:(kc + 1) * NSEQ], kc == 0, kc == 7,
                           r=wkeys(s_) + ["condT"], w=[PSK(bank)])
                    ts(modv[:, l, ej, :], ps[bank][:, 0:NSEQ], adab[:, l * 48 + ej:l * 48 + ej + 1], None, ALU.add, None,
                       r=[PSK(bank), "tables"], w=["modT"])

        rot = {"p": 0}

        def nxt(lo, n):
            v = lo + rot["p"] % n
            rot["p"] += 1
            return v

        def modcol(l, j6, kc, s):
            return modv[:, l, j6 * 8 + kc, s:s + 1]

        def make_norm(l, s, gsrc, j_sc, j_sh):
            for kc in range(8):
                stt(ab[:, kc:kc + 1], modcol(l, j_sc, kc, s), 1.0, gsrc[:, l * 8 + kc:l * 8 + kc + 1], ALU.add, ALU.mult,
                    r=["modT", "tables"], w=["ab"])
            sq = A16(12800 - 2048, 4096).rearrange("p (k c) -> p k c", c=512)
            rstd = A32(12800 - 2048 - 512, 512)

            def p1(b):
                bs = slice(b * 512, (b + 1) * 512)
                act(sq, xT[:, :, bs], AF.Square, r=[("xT", b)], w=["n_sq"])

            def p2(b):
                bs = slice(b * 512, (b + 1) * 512)
                bank = nxt(0, 2)
                for kc in range(8):
                    mm(ps[bank][:], onesb[:], sq[:, kc, :], kc == 0, kc == 7, r=["onesb", "n_sq"], w=[PSK(bank)])
                act(rstd, ps[bank][:], AF.Ln, r=[PSK(bank), "epsc"], w=["n_rstd"], bias=eps_col, scale=1.0 / D)
                act(rstd, rstd, AF.Exp, r=["n_rstd"], w=["n_rstd"], scale=-0.5)
                for kc in range(8):
                    tmp = A32(12800 - 2048 - 512 - 1024 + 512 * (kc % 2), 512)
                    stt(tmp, xT[:, kc, bs], ab[:, kc:kc + 1], rstd, ALU.mult, ALU.mult,
                        r=[("xT", b), "ab", "n_rstd"], w=[("n_tmp", kc % 2)])
                    act(hT[:, kc, bs], tmp, AF.Identity, r=[("n_tmp", kc % 2), "modT"], w=[("hT", b)],
                        bias=modcol(l, j_sh, kc, s), scale=1.0)
            return p1, p2

        def do_norm(l, s, gsrc, j_sc, j_sh, tagr):
            p1, p2 = make_norm(l, s, gsrc, j_sc, j_sh)
            for b in range(NB):
                p1(b)
                p2(b)

        def norm_after_block(nrm, b):
            if nrm is None:
                return
            p1, p2 = nrm
            if b >= 1:
                p2(b - 1)
            p1(b)
            if b == NB - 1:
                p2(b)

        def proj_fm(l, col0, ncols, tokens, r_extra=()):
            raise NotImplementedError

        def load_x(s):
            for t in range(NT):
                buf = A32(1024 * (t % 2), 1024)
                P.dma("sp", buf, x_d[s, t * 128:(t + 1) * 128, :], w=[("xld", t % 2)])
                for g in range(2):
                    bank = nxt(0, 8)
                    for j in range(4):
                        kc = g * 4 + j
                        tr(ps[bank][:, j * 128:(j + 1) * 128], buf[:, kc * 128:(kc + 1) * 128], identf,
                           r=[("xld", t % 2), "cmf"], w=[PSK(bank)])
                    cp(xT[:, g * 4:(g + 1) * 4, t * 128:(t + 1) * 128],
                       ps[bank][:].rearrange("p (j c) -> p j c", c=128), r=[PSK(bank)], w=[("xT", t // 4)],
                       eng=("dve" if (t + g) % 2 == 0 else "act"))

        out_ops = []

        def store_out(s):
            sq = A16(12800 - 2048, 4096).rearrange("p (k c) -> p k c", c=512)
            rstd = A32(12800 - 2048 - 512, 512)
            for b in range(NB):
                bs = slice(b * 512, (b + 1) * 512)
                act(sq, xT[:, :, bs], AF.Square, r=[("xT", b)], w=["n_sq"])
                bank = nxt(0, 2)
                for kc in range(8):
                    mm(ps[bank][:], onesb[:], sq[:, kc, :], kc == 0, kc == 7, r=["onesb", "n_sq"], w=[PSK(bank)])
                act(rstd, ps[bank][:], AF.Ln, r=[PSK(bank), "epsc"], w=["n_rstd"], bias=eps_col, scale=1.0 / D)
                act(rstd, rstd, AF.Exp, r=["n_rstd"], w=["n_rstd"], scale=-0.5)
                yb = A32(0, 4096).rearrange("p (k c) -> p k c", c=512)
                for kc in range(8):
                    stt(yb[:, kc, :], xT[:, kc, bs], gfin[:, kc:kc + 1], rstd, ALU.mult, ALU.mult,
                        r=[("xT", b), "tables", "n_rstd"], w=[("yb", kc)])
                for tl in range(4):
                    t = b * 4 + tl
                    ob = A32(4096 + 1024 * (t % 2), 1024)
                    for g in range(2):
                        bank2 = nxt(2, 6)
                        for j in range(4):
                            kc = g * 4 + j
                            tr(ps[bank2][:, j * 128:(j + 1) * 128], yb[:, kc, tl * 128:(tl + 1) * 128], identf,
                               r=[("yb", kc), "cmf"], w=[PSK(bank2)])
                        cp(ob[:, g * 512:(g + 1) * 512], ps[bank2][:], r=[PSK(bank2)], w=[("ob", t % 2)],
                           eng=("dve" if g == 0 else "act"))
                    out_ops.append(P.dma("sp", out_d[s, t * 128:(t + 1) * 128, :], ob, r=[("ob", t % 2)]))

        def attention(l, s):
            strip = A16(0, STRIP_W)
            QT = [A16(1984, 2048), A16(8896, 2048)]
            KT = A16(3008, 2048)
            ALIAS = ["hid", ("rl", 0), ("rl", 1), ("n_tmp", 0), ("n_tmp", 1), "n_rstd", "n_sq"]
            P.add("dve", lambda e: e.memset(QT[0][64:128, :], 0.0), w=["QT"], war=ALIAS)
            P.add("dve", lambda e: e.memset(QT[1][0:64, :], 0.0), w=["QT"], war=ALIAS)
            V = A16(4032, 2048).rearrange("p (t c) -> p t c", c=128)
            Eb = [A16(5056 + 256 * i, 512) for i in range(2)] + [A16(9920 + 256 * i, 512) for i in range(2)]
            Pm = [A16(5568 + 256 * i, 512) for i in range(2)] + [A16(10432 + 256 * i, 512) for i in range(2)]
            SBK = (2, 3, 0, 1)
            SLA = 3
            nr = [A32(6080 + 512 * i, 512) for i in range(5)]
            sqb = A16(6080 + 512 * 5, 512)
            gcol = small[:, 60:61]
            ts(gcol, subln[:, l:l + 1], 1.0 - lam_init_of(l), None, ALU.mult, None, r=["tables"], w=["gcol"])
            for h in range(4):
                s_ = wload([
                    (lambda sl: kview(sl, 0, 8, 128), wsrc(win_d[l, :, C_AQ + h * 128:C_AQ + (h + 1) * 128])),
                    (lambda sl: kview(sl, 1024, 8, 128), wsrc(win_d[l, :, C_AK + h * 128:C_AK + (h + 1) * 128])),
                    (lambda sl: kview(sl, 2048, 8, 128), wsrc(win_d[l, :, C_AV + h * 128:C_AV + (h + 1) * 128])),
                ])
                wq = kview(wsl[s_], 0, 8, 128)
                wk = kview(wsl[s_], 1024, 8, 128)
                wv_ = kview(wsl[s_], 2048, 8, 128)
                P.dma("pool", strip, strip_d[h], w=["strip"], war=(ALIAS if h == 0 else ()))
                for b in range(NB):
                    bs = slice(b * 512, (b + 1) * 512)
                    bank = nxt(0, 2)
                    for kc in range(8):
                        mm(ps[bank][:], wq[:, kc, :], hT[:, kc, bs], kc == 0, kc == 7, r=wkeys(s_, 0) + [("hT", b)], w=[PSK(bank)])
                    cp(QT[0][0:64, bs], ps[bank][0:64, :], r=[PSK(bank)], w=["QT"], eng="act")
                    cp(QT[1][64:128, bs], ps[bank][64:128, :], r=[PSK(bank)], w=["QT"], eng="dve")
                    bank = nxt(0, 2)
                    for kc in range(8):
                        mm(ps[bank][:], wk[:, kc, :], hT[:, kc, bs], kc == 0, kc == 7, r=wkeys(s_, 1) + [("hT", b)], w=[PSK(bank)])
                    cp(KT[:, bs], ps[bank][:], r=[PSK(bank)], w=["KT"], eng="act")
                    bank = nxt(0, 2)
                    for tl in range(4):
                        t = b * 4 + tl
                        for kc in range(8):
                            mm(ps[bank][:, tl * 128:(tl + 1) * 128], hT[:, kc, t * 128:(t + 1) * 128], wv_[:, kc, :],
                               kc == 0, kc == 7, r=wkeys(s_, 2) + [("hT", b)], w=[PSK(bank)])
                    cp(V[:, b * 4:(b + 1) * 4, :], ps[bank][:].rearrange("p (t c) -> p t c", c=128), r=[PSK(bank)], w=["V"])
                for J in range(NB):
                    qs_ = slice(J * 512, (J + 1) * 512)
                    tiles = attn_tiles(h, J)
                    work = [(i, kt) for i in range(2) for kt in tiles]

                    def s_mm(n):
                        i, kt = work[n]
                        bk = SBK[n % 4]
                        mm(ps[bk][:], KT[:, kt * 128:(kt + 1) * 128], QT[i][:, qs_],
                           True, True, r=["KT", "QT"], w=[PSK(bk)])
                    for n0 in range(min(SLA, len(work))):
                        s_mm(n0)
                    for n, (i, kt) in enumerate(work):
                        if n + SLA < len(work):
                            s_mm(n + SLA)
                        bk = SBK[n % 4]
                        act(Eb[n % 4], ps[bk][:], AF.Exp, r=[PSK(bk)], w=[("E", n % 4)], scale=0.125)
                        off = 512 * J - 128 * kt + DOFF
                        tt(Pm[n % 4], Eb[n % 4], strip[:, off:off + 512], ALU.mult, r=[("E", n % 4), "strip"], w=[("Pm", n % 4)])
                        first = (kt == tiles[0])
                        last = (kt == tiles[-1])
                        mm(ps[4 + 2 * i][:], V[:, kt, :], Pm[n % 4], first, last, r=["V", ("Pm", n % 4)], w=[PSK(4 + 2 * i)])
                        mm(ps[5 + 2 * i][:], onesb[:], Pm[n % 4], first, last, r=["onesb", ("Pm", n % 4)], w=[PSK(5 + 2 * i)])
                    for i in range(2):
                        act(nr[i], ps[5 + 2 * i][:], AF.Ln, r=[PSK(5 + 2 * i)], w=[("nr", i)])
                        act(nr[i], nr[i], AF.Exp, r=[("nr", i)], w=[("nr", i)], scale=-1.0)
                        tt(nr[2 + i], ps[4 + 2 * i][:], nr[i], ALU.mult, r=[PSK(4 + 2 * i), ("nr", i)], w=[("nr", 2 + i)])
                    stt(nr[4], nr[3], neglam[:, l:l + 1], nr[2], ALU.mult, ALU.add, r=[("nr", 3), ("nr", 2), "neglam"], w=[("nr", 4)])
                    tt(sqb, nr[4], nr[4], ALU.mult, r=[("nr", 4)], w=["sqb"])
                    mm(ps[5][:], onesb[:], sqb, True, True, r=["onesb", "sqb"], w=[PSK(5)])
                    act(nr[0], ps[5][:], AF.Ln, r=[PSK(5), "epsc"], w=[("nr", 0)], bias=eps_col, scale=1.0 / 128)
                    act(nr[0], nr[0], AF.Exp, r=[("nr", 0)], w=[("nr", 0)], scale=-0.5)
                    stt(oT[:, h, qs_], nr[4], gcol, nr[0], ALU.mult, ALU.mult, r=[("nr", 4), ("nr", 0), "gcol"], w=["oT"])

        def scan_mixer(l, s, m, c):
            CP = 128 if m == 0 else 64
            dk = CP // 2
            qsT = A16(0, 2048)
            qpad = A16(1024, 4096).rearrange("p (t j c) -> p t j c", j=2, c=128)
            ktil = A16(3072, 2048)
            khat = A16(4096, 2048).rearrange("p (t c) -> p t c", c=128)
            V = A16(5120, 2048).rearrange("p (t c) -> p t c", c=128)
            oacc = A32(6144, 2048).rearrange("p (t c) -> p t c", c=128)
            dec = A32(8192, 32)
            emid = A32(8224, 40)
            Sst = A32(8320, 128)
            tmpU = A32(8448, 128)
            Sbf = A16(8576, 128)
            Am = [A16(8640 + 128 * i, 256) for i in range(2)]
            BT = [A32(8960 + 512 * i, 512) for i in range(7)]
            BD = BDh if m == 0 else BDg
            scale_e = 1.0 if m == 0 else -1.0 / 16.0
            if m == 0:
                cq, cv, cg = C_BQ + c * 128, C_BI + c * 128, C_BG + c * 128
                qscale = 0.125
            else:
                cq, cv, cg = C_CQ + c * 64, C_CV + c * 128, C_CG + c * 128
                qscale = 32 ** -0.5
            sA = wload([
                (lambda sl: kview(sl, 0, 8, CP), wsrc(win_d[l, :, cq:cq + CP])),
                (lambda sl: kview(sl, 1024, 8, 128), wsrc(win_d[l, :, cv:cv + 128])),
                (lambda sl: kview(sl, 2048, 8, 128), wsrc(win_d[l, :, cg:cg + 128])),
            ])
            wq = kview(wsl[sA], 0, 8, CP)
            wv_ = kview(wsl[sA], 1024, 8, 128)
            wg = kview(wsl[sA], 2048, 8, 128)
            if m == 0:
                sB = wload([
                    (lambda sl: kview(sl, 0, 8, 128), wsrc(win_d[l, :, C_BFF + c * 128:C_BFF + (c + 1) * 128])),
                    (lambda sl: kview(sl, 1024, 8, 128), wsrc(win_d[l, :, C_BFB + c * 128:C_BFB + (c + 1) * 128])),
                ])
                wz = [kview(wsl[sB], 0, 8, 128), kview(wsl[sB], 1024, 8, 128)]
                wkk = None
                wlr = None
            else:
                sB = wload([
                    (lambda sl: kview(sl, 0, 8, 64), wsrc(win_d[l, :, C_CK + c * 64:C_CK + (c + 1) * 64])),
                    (lambda sl: kview(sl, 1024, 8, 16), wsrc(win_d[l, :, C_LRF:C_LRF + 16])),
                    (lambda sl: kview(sl, 2048, 8, 16), wsrc(win_d[l, :, C_LRB:C_LRB + 16])),
                ])
                wkk = kview(wsl[sB], 0, 8, 64)
                wlr = [kview(wsl[sB], 1024, 8, 16), kview(wsl[sB], 2048, 8, 16)]
            for b in range(NB):
                bs = slice(b * 512, (b + 1) * 512)
                bank = nxt(0, 2)
                for kc in range(8):
                    mm(ps[bank][0:CP, :], wq[:, kc, :], hT[:, kc, bs], kc == 0, kc == 7, r=wkeys(sA, 0) + [("hT", b)], w=[PSK(bank)])
                if m == 0:
                    act(qsT[0:CP, bs], ps[bank][0:CP, :], AF.Silu, r=[PSK(bank)], w=["qsT"])
                else:
                    cp(qsT[0:CP, bs], ps[bank][0:CP, :], r=[PSK(bank)], w=["qsT"], eng="act")
                bank = nxt(0, 2)
                for tl in range(4):
                    t = b * 4 + tl
                    for kc in range(8):
                        mm(ps[bank][:, tl * 128:(tl + 1) * 128], hT[:, kc, t * 128:(t + 1) * 128], wv_[:, kc, :],
                           kc == 0, kc == 7, r=wkeys(sA, 1) + [("hT", b)], w=[PSK(bank)])
                cp(V[:, b * 4:(b + 1) * 4, :], ps[bank][:].rearrange("p (t c) -> p t c", c=128), r=[PSK(bank)], w=["sV"])
            P.add("dve", lambda e: e.memset(qpad[0:CP], 0.0), w=["qpad"])

            for dr in range(2):
                endp, midp = (63, 31) if dr == 0 else (0, 32)
                lidx = (l * 2 + dr) * 2 + c
                def prep_chain(b, hf):
                    NTK = 256
                    t0 = b * 512 + hf * NTK
                    hs = slice(t0, t0 + NTK)
                    fs = slice(hf * NTK, (hf + 1) * NTK)
                    la, Bc, d1, e1, e2, e3, kk = [x[:, fs] for x in BT]
                    K_ = lambda i: ("bt", i, hf)
                    bank = hf
                    nch = NTK // 64
                    if m == 0:
                        for kc in range(8):
                            mm(ps[bank][:, 0:NTK], wz[dr][:, kc, :], hT[:, kc, hs], kc == 0, kc == 7, r=wkeys(sB, dr) + [("hT", b)], w=[PSK(bank)])
                        yield
                        act(e1, ps[bank][:, 0:NTK], AF.Exp, r=[PSK(bank)], w=[K_(3)], scale=-1.0)
                        yield
                        act(e2, e1, AF.Ln, r=[K_(3), "epsc"], w=[K_(4)], bias=one_col, scale=1.0)
                        yield
                        act(e2, e2, AF.Exp, r=[K_(4)], w=[K_(4)], scale=-1.0)
                        yield
                        act(la, e2, AF.Ln, r=[K_(4), "lbv", "omlv"], w=[K_(0)],
                            bias=lbv[:, lidx:lidx + 1], scale=omlv[:, lidx:lidx + 1])
                        stt(kk, e1, omlv[:, lidx:lidx + 1], e2, ALU.mult, ALU.mult, r=[K_(3), K_(4), "omlv"], w=[K_(6)])
                        yield
                    else:
                        for kc in range(8):
                            mm(ps[bank][0:16, 0:NTK], wlr[dr][:, kc, :], hT[:, kc, hs], kc == 0, kc == 7, r=wkeys(sB, 1 + dr) + [("hT", b)], w=[PSK(bank)])
                        yield
                        lrT = d1[:, 0:NTK // 2].bitcast(BF16)
                        cp(lrT[0:16, :], ps[bank][0:16, 0:NTK], r=[PSK(bank)], w=[K_(2)])
                        yield
                        goff = (l * 2 + dr) * 128 + c * 64
                        mm(ps[bank][0:CP, 0:NTK], gw2[:, goff:goff + 64], lrT[0:16, :], True, True, r=["gw2", K_(2)], w=[PSK(bank)])
                        yield
                        act(e1[0:CP, :], ps[bank][0:CP, 0:NTK], AF.Exp, r=[PSK(bank), "ngb"], w=[K_(3)],
                            bias=ngb[0:CP, lidx:lidx + 1], scale=-1.0)
                        yield
                        act(la[0:CP, :], e1[0:CP, :], AF.Ln, r=[K_(3), "epsc"], w=[K_(0)], bias=one_col[0:CP], scale=1.0)
                        for kc in range(8):
                            mm(ps[bank][0:CP, 0:NTK], wkk[:, kc, :], hT[:, kc, hs], kc == 0, kc == 7, r=wkeys(sB, 0) + [("hT", b)], w=[PSK(bank)])
                        yield
                        cp(kk[0:CP, :], ps[bank][0:CP, 0:NTK], r=[PSK(bank)], w=[K_(6)], eng="act")
                        yield
                    P.add("dve", lambda e, Bc=Bc, la=la: e.tensor_tensor_scan(out=Bc[0:CP, :], data0=rstb[0:CP, 0:NTK], data1=la[0:CP, :],
                                                                             initial=0.0, op0=ALU.mult, op1=ALU.add),
                          r=["rstb", K_(0)], w=[K_(1)])
                    yield
                    B3 = Bc[0:CP, :].rearrange("p (c t) -> p c t", t=64)
                    la3 = la[0:CP, :].rearrange("p (c t) -> p c t", t=64)
                    d13 = d1[0:CP, :].rearrange("p (c t) -> p c t", t=64)
                    bkey, fkey, fbuf = K_(1), K_(0), la
                    if dr == 1:
                        tt(d13, la3, B3, ALU.subtract, r=[K_(0), K_(1)], w=[K_(2)])
                        yield
                        tt(la3, d13, B3[:, :, 63:64].to_broadcast([CP, nch, 64]), ALU.add,
                           r=[K_(2), K_(1)], w=[K_(0)])
                        yield
                        B3 = la3
                        bkey, fkey, fbuf = K_(0), K_(1), Bc
                    tt(d13, B3, B3[:, :, midp:midp + 1].to_broadcast([CP, nch, 64]), ALU.subtract, r=[bkey], w=[K_(2)])
                    yield
                    act(e1[0:CP, :], d1[0:CP, :], AF.Exp, r=[K_(2)], w=[K_(3)], scale=scale_e)
                    act(e2[0:CP, :], d1[0:CP, :], AF.Exp, r=[K_(2)], w=[K_(4)], scale=-scale_e)
                    yield
                    tt(d13, B3[:, :, endp:endp + 1].to_broadcast([CP, nch, 64]), B3, ALU.subtract, r=[bkey], w=[K_(2)])
                    yield
                    act(e3[0:CP, :], d1[0:CP, :], AF.Exp, r=[K_(2)], w=[K_(5)], scale=scale_e)
                    c0 = b * 8 + hf * nch
                    act(dec[0:CP, c0:c0 + nch], B3[:, :, endp], AF.Exp, r=[bkey], w=["dec"], scale=scale_e)
                    act(emid[0:CP, c0:c0 + nch], B3[:, :, midp], AF.Exp, r=[bkey], w=["emid"], scale=scale_e)
                    tl0 = b * 4 + hf * 2
                    qp = qpad[0:CP, tl0:tl0 + 2, :, :]
                    for j in range(2):
                        stt(qp[:, :, j, 64 * j:64 * j + 64], qsT[0:CP, hs].rearrange("p (t j c) -> p t j c", j=2, c=64)[:, :, j, :], qscale,
                            e1[0:CP, :].rearrange("p (t j c) -> p t j c", j=2, c=64)[:, :, j, :], ALU.mult, ALU.mult,
                            r=["qsT", K_(3)], w=["qpad"])
                    tt(ktil[0:CP, hs], kk[0:CP, :], e2[0:CP, :], ALU.mult, r=[K_(6), K_(4)], w=["ktil"])
                    yield
                    khT = fbuf[:, 0:NTK // 2].bitcast(BF16)
                    tt(khT[0:CP, :], kk[0:CP, :], e3[0:CP, :], ALU.mult, r=[K_(6), K_(5)], w=[fkey])
                    yield
                    pbf = ps[bank][:].bitcast(BF16)
                    for tl in range(2):
                        tr(pbf[:, tl * 128:tl * 128 + CP], khT[0:CP, tl * 128:(tl + 1) * 128], identb[0:CP, 0:CP],
                           r=[fkey, "identb"], w=[PSK(bank)])
                    yield
                    cp(khat[:, tl0:tl0 + 2, 0:CP], pbf[:, 0:256].rearrange("p (t c) -> p t c", c=128)[:, :, 0:CP],
                       r=[PSK(bank)], w=["khat"], eng="act")
                    yield

                H_ = (8 if m == 0 else 10) + dr
                gens = [prep_chain(b, hf) for b in range(NB) for hf in range(2)]
                done = [False] * len(gens)
                tstep = 0
                while not all(done):
                    for gi, g in enumerate(gens):
                        if done[gi] or tstep < gi * H_:
                            continue
                        try:
                            next(g)
                        except StopIteration:
                            done[gi] = True
                    tstep += 1
                P.add("dve", lambda e: e.memset(Sst[0:CP, :], 0.0), w=["Sst"])
                P.add("dve", lambda e: e.memset(Sbf[0:CP, :], 0.0), w=[("Sbf", 0)])
                for i_ in range(2):
                    P.add("dve", lambda e, i_=i_: e.memset(Am[i_], 0.0), w=[("Am", i_)])
                mki = mk2[dr][:].bitcast(mybir.dt.int32)
                torder = list(range(NT)) if dr == 0 else list(range(NT - 1, -1, -1))
                jorder = (0, 1) if dr == 0 else (1, 0)
                chunks = [(t, j) for t in torder for j in jorder]

                def a_mm(ti):
                    t = torder[ti]
                    bk = 2 + ti % 2
                    for hh in range(2):
                        rows = slice(dk * hh, dk * hh + dk)
                        for j in range(2):
                            rr = ["ktil", "qpad"] + ([PSK(bk)] if (hh == 1 and j == 0) else [])
                            mm(ps[bk][:, hh * 128 + 64 * j:hh * 128 + 64 * j + 64], ktil[rows, t * 128:(t + 1) * 128],
                               qpad[rows, t, j, 64 * j:64 * j + 64], True, True, r=rr, w=[PSK(bk)])

                def qplain(rows, t):
                    v = qpad[rows, t, :, :]
                    return v.rearrange("p j (a c) -> p (j a) c", a=2)[:, 0:4:3, :]

                R_ = 3
                LA_ = 2
                tmpUs = [A32(8448, 128), A32(12544, 128), A32(12672, 128)]
                Sbfs = [Sbf, A16(8896, 128)]
                P.add("dve", lambda e: e.memset(Sbfs[1][0:CP, :], 0.0), w=[("Sbf", 1)])

                def issue_U(ci):
                    t_, j_ = chunks[ci]
                    ub = 6 + ci % 2
                    mm(ps[ub][0:CP, 0:128], khat[64 * j_:64 * j_ + 64, t_, 0:CP], V[64 * j_:64 * j_ + 64, t_, :], True, True,
                       r=["khat", "sV"], w=[PSK(ub)])

                a_mm(0)
                for ci0 in range(LA_):
                    issue_U(ci0)
                for ti, t in enumerate(torder):
                    if ti + 1 < NT:
                        a_mm(ti + 1)
                    bk = 2 + ti % 2
                    am = Am[ti % 2]
                    P.add("dve", lambda e, am=am, bk=bk, mki=mki: e.copy_predicated(out=am, mask=mki, data=ps[bk][:, 0:256]),
                          r=[PSK(bk), "mk2", ("Am", ti % 2)], w=[("Am", ti % 2)])
                    ob = 4 + ti % 2
                    for ji, j in enumerate(jorder):
                        ci = ti * 2 + ji
                        cidx = t * 2 + j
                        sbf = Sbfs[ci % 2]
                        for hh in range(2):
                            rows = slice(dk * hh, dk * hh + dk)
                            rr = ["qpad", ("Sbf", ci % 2)] + ([PSK(ob)] if hh == 1 else [])
                            mm(ps[ob][:, hh * 64:(hh + 1) * 64], qpad[rows, t, j, :], sbf[rows, hh * 64:(hh + 1) * 64],
                               (ji == 0 and hh == 0), (ji == 1 and hh == 1), r=rr, w=[PSK(ob)], skip=True)
                        if ji == 0:
                            for hh in range(2):
                                mm(ps[ob][:, hh * 64:(hh + 1) * 64], am[:, hh * 128:(hh + 1) * 128], V[:, t, hh * 64:(hh + 1) * 64],
                                   False, False, r=[("Am", ti % 2), "sV"], w=[PSK(ob)], skip=True)
                        ub_ = 6 + ci % 2
                        stt(Sst[0:CP, :], Sst[0:CP, :], dec[0:CP, cidx:cidx + 1], ps[ub_][0:CP, 0:128], ALU.mult, ALU.add,
                            r=["Sst", "dec", PSK(ub_)], w=["Sst"])
                        if ci + LA_ < len(chunks):
                            issue_U(ci + LA_)
                        if ci + 1 < len(chunks):
                            tn, jn = chunks[ci + 1]
                            nidx = tn * 2 + jn
                            act(Sbfs[(ci + 1) % 2][0:CP, :], Sst[0:CP, :], AF.Copy, r=["Sst", "emid"], w=[("Sbf", (ci + 1) % 2)],
                                scale=emid[0:CP, nidx:nidx + 1])
                    if dr == 0:
                        cp(oacc[:, t, :], ps[ob][:, 0:128], r=[PSK(ob)], w=["oacc"], eng="act")
                    else:
                        tt(oacc[:, t, :], oacc[:, t, :], ps[ob][:, 0:128], ALU.add, r=[PSK(ob), "oacc"], w=["oacc"])
                P.barrier()
            P.barrier()
            gB = (hgB if m == 0 else ggB)[:, l * 64:(l + 1) * 64]
            osq = A32(8960, 2048)
            ss = A32(8960 + 2048, 32)
            on = A16(8960 + 2048 + 64, 2048)
            o2 = oacc.rearrange("p t c -> p (t c)")
            tt(osq, o2, o2, ALU.mult, r=["oacc"], w=["osq"])
            P.add("dve", lambda e: e.tensor_reduce(out=ss, in_=osq.rearrange("p (a d) -> p a d", d=64), axis=AX.X, op=ALU.add),
                  r=["osq"], w=["ss"])
            act(ss, ss, AF.Ln, r=["ss", "epsc"], w=["ss"], bias=eps_col, scale=1.0 / 64)
            act(ss, ss, AF.Exp, r=["ss"], w=["ss"], scale=-0.5)
            o3 = o2.rearrange("p (a d) -> p a d", d=64)
            tt(o3, o3, ss.unsqueeze(2).to_broadcast([128, 32, 64]), ALU.mult, r=["oacc", "ss"], w=["oacc"])
            tt(on.rearrange("p (a d) -> p a d", d=64), o3, gB.unsqueeze(1).to_broadcast([128, 32, 64]), ALU.mult,
               r=["oacc", "tables"], w=["on"])
            och = 4 + 2 * m + c
            for b in range(NB):
                bs = slice(b * 512, (b + 1) * 512)
                bank = nxt(0, 2)
                for kc in range(8):
                    mm(ps[bank][:], wg[:, kc, :], hT[:, kc, bs], kc == 0, kc == 7, r=wkeys(sA, 2) + [("hT", b)], w=[PSK(bank)])
                gT = A32(12288, 512)
                act(gT, ps[bank][:], AF.Silu, r=[PSK(bank)], w=["gT"])
                bank = nxt(0, 2)
                pbf = ps[bank][:].bitcast(BF16)
                for tl in range(4):
                    t = b * 4 + tl
                    tr(pbf[:, tl * 128:(tl + 1) * 128], on[:, t * 128:(t + 1) * 128], identb[:], r=["on", "identb"], w=[PSK(bank)])
                tt(oT[:, och, bs], pbf[:, 0:512], gT, ALU.mult, r=[PSK(bank), "gT"], w=["oT"])

        def merge_and_out(l, s, nrm_args=None):
            mT = A16(0, 16384).rearrange("p (k c) -> p k c", c=S)
            sg = [A32(8192 + 512 * i, 512) for i in range(3)]
            mt = [A32(8192 + 1536 + 512 * i, 512) for i in range(2)]
            ksl = (slice(0, 4), slice(4, 6), slice(6, 8))
            for ec in range(8):
                s_ = wload([
                    (lambda sl: kview(sl, 0, 8, 128), wsrc(win_d[l, :, C_GATE + ec * 128:C_GATE + (ec + 1) * 128])),
                    (lambda sl: kview(sl, 1024, 8, 128), wsrc(win_d[l, :, C_GATE + D + ec * 128:C_GATE + D + (ec + 1) * 128])),
                    (lambda sl: kview(sl, 2048, 8, 128), wsrc(win_d[l, :, C_GATE + 2 * D + ec * 128:C_GATE + 2 * D + (ec + 1) * 128])),
                    (lambda sl: kview(sl, 3072, 8, 128), wsrc(wup_d[l, :, ec * 128:(ec + 1) * 128])),
                ])
                wgt = [kview(wsl[s_], 1024 * i, 8, 128) for i in range(3)]
                wu = kview(wsl[s_], 3072, 8, 128)
                for b in range(NB):
                    bs = slice(b * 512, (b + 1) * 512)
                    for br in range(3):
                        for kc in range(8):
                            mm(ps[br][:], wgt[br][:, kc, :], hT[:, kc, bs], kc == 0, kc == 7, r=wkeys(s_, br) + [("hT", b)], w=[PSK(br)])
                        act(sg[br], ps[br][:], AF.Sigmoid, r=[PSK(br)], w=[("sg", br)])
                    for br in range(3):
                        kcs = list(range(8))[ksl[br]]
                        for kc in kcs:
                            mm(ps[3 + br][:], wu[:, kc, :], oT[:, kc, bs], kc == kcs[0], kc == kcs[-1], r=wkeys(s_, 3) + ["oT"], w=[PSK(3 + br)])
                    tt(mt[0], ps[3][:], sg[0], ALU.mult, r=[PSK(3), ("sg", 0)], w=[("mt", 0)])
                    tt(mt[1], ps[4][:], sg[1], ALU.mult, r=[PSK(4), ("sg", 1)], w=[("mt", 1)])
                    tt(mt[0], mt[0], mt[1], ALU.add, r=[("mt", 0), ("mt", 1)], w=[("mt", 0)])
                    tt(mt[1], ps[5][:], sg[2], ALU.mult, r=[PSK(5), ("sg", 2)], w=[("mt", 1)])
                    tt(mT[:, ec, bs], mt[0], mt[1], ALU.add, r=[("mt", 0), ("mt", 1)], w=["mT"])
            for eg in range(2):
                s_ = wload([(lambda sl: kview(sl, 0, 8, 512), wsrc(wout_d[l, :, eg * 512:(eg + 1) * 512]))])
                wo = kview(wsl[s_], 0, 8, 512)
                nrm = make_norm(*nrm_args) if (eg == 1 and nrm_args is not None) else None
                for b in range(NB):
                    bs = slice(b * 512, (b + 1) * 512)
                    for e4 in range(4):
                        ec = eg * 4 + e4
                        bank = nxt(0, 8)
                        for kc in range(8):
                            mm(ps[bank][:], wo[:, kc, e4 * 128:(e4 + 1) * 128], mT[:, kc, bs], kc == 0, kc == 7,
                               r=wkeys(s_) + ["mT"], w=[PSK(bank)])
                        stt(xT[:, ec, bs], ps[bank][:], modcol(l, 2, ec, s), xT[:, ec, bs], ALU.mult, ALU.add,
                            r=[PSK(bank), "modT", ("xT", b)], w=[("xT", b)])
                    norm_after_block(nrm, b)

        def mlp(l, s, nrm_args=None):
            hid = A16(0, 8192).rearrange("p (k c) -> p k c", c=S)
            rl = [A32(4096 + 512 * i, 512) for i in range(2)]
            for fg in range(8):
                s1 = wload([(lambda sl: kview(sl, 0, 8, 512), wsrc(w1_d[l, :, fg * 512:(fg + 1) * 512]))])
                w1v = kview(wsl[s1], 0, 8, 512)
                n = 0
                for fc in range(4):
                    for b in range(NB):
                        bs = slice(b * 512, (b + 1) * 512)
                        bank = nxt(0, 8)
                        for kc in range(8):
                            mm(ps[bank][:], w1v[:, kc, fc * 128:(fc + 1) * 128], hT[:, kc, bs], kc == 0, kc == 7,
                               r=wkeys(s1) + [("hT", b)], w=[PSK(bank)])
                        act(rl[n % 2], ps[bank][:], AF.Relu, r=[PSK(bank)], w=[("rl", n % 2)])
                        tt(hid[:, fc, bs], rl[n % 2], rl[n % 2], ALU.mult, r=[("rl", n % 2)], w=["hid"])
                        n += 1
                s2 = wload([(lambda sl: kview(sl, 0, 4, 1024), w2_d[l, fg * 512:(fg + 1) * 512, :].rearrange("(k p) c -> p k c", p=128))])
                w2v = kview(wsl[s2], 0, 4, 1024)
                nrm = make_norm(*nrm_args) if (fg == 7 and nrm_args is not None) else None
                for b in range(NB):
                    bs = slice(b * 512, (b + 1) * 512)
                    for ec in range(8):
                        bank = nxt(0, 8)
                        for fc in range(4):
                            mm(ps[bank][:], w2v[:, fc, ec * 128:(ec + 1) * 128], hid[:, fc, bs], fc == 0, fc == 3,
                               r=wkeys(s2) + ["hid"], w=[PSK(bank)])
                        stt(xT[:, ec, bs], ps[bank][:], modcol(l, 5, ec, s), xT[:, ec, bs], ALU.mult, ALU.add,
                            r=[PSK(bank), "modT", ("xT", b)], w=[("xT", b)])
                    norm_after_block(nrm, b)

        for s in range(NSEQ):
            P.barrier()
            load_x(s)
            for l in range(DEPTH):
                full = all(p in PH for p in ("attn", "scan", "merge", "mlp"))
                if l == 0:
                    P.barrier()
                    do_norm(l, s, gmix, 1, 0, "mix")
                elif not full:
                    do_norm(l, s, gmix, 1, 0, "mix")
                if "attn" in PH:
                    attention(l, s)
                for m in range(2):
                    for c in range(2):
                        if "scan" in PH:
                            P.barrier()
                            scan_mixer(l, s, m, c)
                P.barrier()
                if dbg == 1:
                    P.barrier()
                    for kc in range(8):
                        cp(xT[:, kc, :], oT[:, kc, :], r=["oT"], w=[("xT", 0), ("xT", 1), ("xT", 2), ("xT", 3)])
                if "merge" in PH:
                    merge_and_out(l, s, (l, s, gmlp, 4, 3) if full else None)
                if "mlp" in PH:
                    if not full:
                        do_norm(l, s, gmlp, 4, 3, "mlp")
                    mlp(l, s, (l + 1, s, gmix, 1, 0) if (full and l + 1 < DEPTH) else None)
            P.barrier()
            if dbg:
                out_ops.append(P.dma("sp", dbg_d, xT[:], r=[("xT", 0), ("xT", 1), ("xT", 2), ("xT", 3), "oT"]))
            store_out(s)
        P.emit(final_wait_ops=[o for o in out_ops if o is not None])
    return nc, P


def const_tables():
    cm = np.zeros((128, 1408), np.float32)
    cm[:, 0:128] = np.eye(128, dtype=np.float32)
    si = np.arange(128)[:, None]
    ti = np.arange(128)[None, :]
    same = (si // 64) == (ti // 64)
    cm[:, 128:256] = (same & (si <= ti)).astype(np.float32)
    cm[:, 256:384] = (same & (si >= ti)).astype(np.float32)
    cm[:, 384:512] = ((si // 64) == (ti // 64)).astype(np.float32)
    cm[:, 512:640] = ((si // 32) == (ti // 64)).astype(np.float32)
    rst = np.ones((128, 512), np.float32)
    rst[:, 0::64] = 0.0
    cm[:, 640:1152] = rst
    cm[:, 1152:1280] = 1.0
    strips = np.zeros((4, 128, STRIP_W), np.float32)
    p = np.arange(128, dtype=np.float64)[:, None]
    u = np.arange(STRIP_W, dtype=np.float64)[None, :]
    for h in range(4):
        strips[h] = np.exp(-SLOPES[h] * np.abs(u - p - DOFF)).astype(np.float32)
    return cm, strips


def colT(v, nchunk):
    lead = v.shape[:-1]
    a = v.reshape(lead + (nchunk, 128))
    a = np.moveaxis(a, -1, 0)
    return np.ascontiguousarray(a.reshape(128, -1))


def make_in_maps(inputs, ncores, nseq, depth):
    f = lambda a: np.ascontiguousarray(np.asarray(a, dtype=np.float32))
    x = f(inputs["x"])
    c = f(inputs["c"])
    cm, strips = const_tables()
    w_up = np.concatenate([f(inputs["w_up_a"])[:depth], f(inputs["w_up_b"])[:depth], f(inputs["w_up_c"])[:depth]], axis=1)
    gb = f(inputs["gla_gate_b"])[:depth]
    gbT = np.zeros((128, depth * 4), np.float32)
    gbT[:64] = np.moveaxis(gb.reshape(depth, 2, 2, 64), -1, 0).reshape(64, -1)
    shared = {
        "ada_w": f(inputs["ada_w"])[:depth],
        "ada_bT": colT(f(inputs["ada_b"])[:depth], 48),
        "w_in": f(inputs["w_in"])[:depth],
        "w_up": np.ascontiguousarray(w_up),
        "w_out": f(inputs["w_out"])[:depth],
        "mlp_w1": f(inputs["mlp_w1"])[:depth],
        "mlp_w2": f(inputs["mlp_w2"])[:depth],
        "gmixT": colT(f(inputs["norm_mix_g"])[:depth], 8),
        "gmlpT": colT(f(inputs["norm_mlp_g"])[:depth], 8),
        "gfinT": colT(f(inputs["final_norm_g"]), 8),
        "sublnT": colT(f(inputs["diff_subln_g"])[:depth], 1),
        "lamb": np.ascontiguousarray(np.broadcast_to(f(inputs["diff_lambda"])[:depth].reshape(1, -1), (128, depth * 256))),
        "lbT": colT(f(inputs["hgrn_lb_logits"])[:depth], 2),
        "hgB": np.ascontiguousarray(np.broadcast_to(f(inputs["hgrn_norm_g"])[:depth].reshape(1, -1), (128, depth * 64))),
        "ggB": np.ascontiguousarray(np.broadcast_to(f(inputs["gla_norm_g"])[:depth].reshape(1, -1), (128, depth * 64))),
        "gbT": gbT,
        "gw2": np.ascontiguousarray(np.moveaxis(f(inputs["gla_gate_w2"])[:depth], 2, 0).reshape(16, -1)),
        "strips": strips,
        "cmasks": cm,
    }
    maps = []
    for i in range(ncores):
        m = dict(shared)
        m["x"] = np.ascontiguousarray(x[i * nseq:(i + 1) * nseq])
        m["cT"] = colT(c[i * nseq:(i + 1) * nseq], 8)
        cc = c[i * nseq:(i + 1) * nseq].reshape(nseq, 8, 128)
        m["cT"] = np.ascontiguousarray(np.transpose(cc, (2, 1, 0)).reshape(128, 8 * nseq))
        maps.append(m)
    return maps


_CACHE = {}


def kernel(**inputs):
    ncores, nseq, depth = 8, 4, 4
    if "nc" not in _CACHE:
        _CACHE["nc"] = build_program(nseq, depth)[0]
    nc = _CACHE["nc"]
    maps = make_in_maps(inputs, ncores, nseq, depth)
    res = run_bass_kernel_spmd(nc, maps, core_ids=list(range(ncores)))
    out = np.concatenate([r["out"] for r in res.results], axis=0)
    return out.astype(np.float32)
```
